# Optimizing a Trainium2 kernel written in Bass

```python
import jax, jax.numpy as jnp
from jax import lax
import numpy as np

D_MODEL = 1024
BATCH = 4
SEQ = 8192
DEPTH = 4

GRID_W = 64
CTX_LEN = 256
LRU_WIDTH = 512
LRU_HEADS = 8
LRU_BLOCK = LRU_WIDTH // LRU_HEADS
LRU_CONV = 4
LRU_C = 8.0
MLA_HEADS = 8
MLA_Q_RANK = 384
MLA_KV_RANK = 256
MLA_NOPE = 64
MLA_ROPE = 32
MLA_QK = MLA_NOPE + MLA_ROPE
MLA_V = 64
CONF_WIDTH = 512
CONF_K = 31
FFN_HIDDEN = 4 * D_MODEL
N_BRANCH = 3
Q_BLOCK = 128
ROPE_THETA = 10000.0
EPS = 1e-6
IN_SPLITS = (LRU_WIDTH, LRU_WIDTH, MLA_Q_RANK, MLA_KV_RANK, MLA_ROPE, 2 * CONF_WIDTH, N_BRANCH * D_MODEL)
D_IN = 512 + 512 + 384 + 256 + 32 + 1024 + 3072

kernel_name = "hybrid_lru_mla_conformer_dit"


def rms_norm(x, gain=None):
    xf = x.astype(jnp.float32)
    y = (xf * lax.rsqrt(jnp.mean(xf * xf, axis=-1, keepdims=True) + EPS)).astype(x.dtype)
    return y if gain is None else y * gain


def layer_norm(x, g, b):
    xf = x.astype(jnp.float32)
    mu = jnp.mean(xf, axis=-1, keepdims=True)
    var = jnp.mean(jnp.square(xf - mu), axis=-1, keepdims=True)
    return ((xf - mu) * lax.rsqrt(var + EPS)).astype(x.dtype) * g + b


def modulate(h, shift, scale):
    return h * (1 + scale) + shift


def split_in(p):
    outs, o = [], 0
    for w in IN_SPLITS:
        outs.append(p[..., o:o + w])
        o += w
    return outs


def dwconv(u, w, b, pad_l, pad_r):
    out = lax.conv_general_dilated(u, w[:, None, :], window_strides=(1,), padding=[(pad_l, pad_r)],
                                   dimension_numbers=('NWC', 'WIO', 'NWC'), feature_group_count=u.shape[-1])
    return out + b


def axial_rope_tables(seq):
    rows = seq // GRID_W
    row = jnp.repeat(jnp.arange(rows, dtype=jnp.float32), GRID_W)
    col = jnp.tile(jnp.arange(GRID_W, dtype=jnp.float32), rows)
    half = MLA_ROPE // 2
    freqs = ROPE_THETA ** (-jnp.arange(0, half, 2, dtype=jnp.float32) / half)
    ang = jnp.concatenate([row[:, None] * freqs, col[:, None] * freqs], axis=-1)
    return jnp.cos(ang), jnp.sin(ang)


def apply_rope(x, cos, sin):
    c = cos[None, :, None, :].astype(x.dtype)
    s = sin[None, :, None, :].astype(x.dtype)
    x1, x2 = x[..., :MLA_ROPE // 2], x[..., MLA_ROPE // 2:]
    return jnp.concatenate([x1 * c - x2 * s, x1 * s + x2 * c], axis=-1)


def rglru_coeffs(u, wa, ba, wx, bx, lam):
    uh = u.reshape(u.shape[:-1] + (LRU_HEADS, LRU_BLOCK))
    r = jax.nn.sigmoid(jnp.einsum('bthi,hij->bthj', uh, wa).reshape(u.shape) + ba)
    i = jax.nn.sigmoid(jnp.einsum('bthi,hij->bthj', uh, wx).reshape(u.shape) + bx)
    log_a = -LRU_C * r * jax.nn.softplus(-lam)
    a = jnp.exp(log_a)
    return a, jnp.sqrt(-jnp.expm1(2 * log_a)) * (i * u)


def linear_scan(a, b, reverse):
    def comb(l, r):
        return (l[0] * r[0], r[0] * l[1] + r[1])
    return lax.associative_scan(comb, (a, b), reverse=reverse, axis=1)


def rglru_bidir(u_x, P, h0_f, h0_b):
    u = dwconv(u_x, P['lru_conv_w'], P['lru_conv_b'], LRU_CONV // 2, LRU_CONV - 1 - LRU_CONV // 2)
    hs = []
    for d, (h0, rev) in enumerate(((h0_f, False), (h0_b, True))):
        a, b = rglru_coeffs(u, P['lru_wa'][d], P['lru_ba'][d], P['lru_wx'][d], P['lru_bx'][d], P['lru_lambda'][d])
        A, H = linear_scan(a, b, rev)
        hs.append(H + A * h0[:, None, :])
    return hs[0], hs[1]


def mla_qkv(cq, ckv, krope, P, cos, sin):
    lead = cq.shape[:-1]
    q = (rms_norm(cq, P['mla_q_norm']) @ P['mla_w_uq']).reshape(lead + (MLA_HEADS, MLA_QK))
    kv = (rms_norm(ckv, P['mla_kv_norm']) @ P['mla_w_ukv']).reshape(lead + (MLA_HEADS, MLA_NOPE + MLA_V))
    k_nope, v = kv[..., :MLA_NOPE], kv[..., MLA_NOPE:]
    k_r = jnp.broadcast_to(krope[:, :, None, :], lead + (MLA_HEADS, MLA_ROPE))
    k = jnp.concatenate([k_nope, k_r], axis=-1)
    q = rms_norm(q, P['mla_q_gain'])
    k = rms_norm(k, P['mla_k_gain'])
    if cos is not None:
        q = jnp.concatenate([q[..., :MLA_NOPE], apply_rope(q[..., MLA_NOPE:], cos, sin)], axis=-1)
        k = jnp.concatenate([k[..., :MLA_NOPE], apply_rope(k[..., MLA_NOPE:], cos, sin)], axis=-1)
    return q, k, v


def attend_blocks(q, k, v):
    bsz, t = q.shape[0], q.shape[1]
    nb = t // Q_BLOCK
    qb = q.reshape(bsz, nb, Q_BLOCK, MLA_HEADS, MLA_QK).transpose(1, 0, 2, 3, 4)
    scale = MLA_QK ** -0.5

    def one(qblk):
        s = jnp.einsum('bqhd,bkhd->bhqk', qblk, k).astype(jnp.float32) * scale
        p = jax.nn.softmax(s, axis=-1).astype(v.dtype)
        return jnp.einsum('bhqk,bkhd->bqhd', p, v)

    o = lax.map(one, qb)
    return o.transpose(1, 0, 2, 3, 4).reshape(bsz, t, MLA_HEADS * MLA_V)


def conformer_branch(u, P):
    a, g = jnp.split(u, 2, axis=-1)
    h = a * jax.nn.sigmoid(g)
    h = dwconv(h, P['conf_dw_w'], P['conf_dw_b'], CONF_K // 2, CONF_K // 2)
    h = jax.nn.silu(layer_norm(h, P['conf_ln_g'], P['conf_ln_b']))
    return h @ P['w_conf_o']


def merge(hf, hb, lgate, attn, conv_in, gate_logits, P):
    y_a = ((hf + hb) * jax.nn.gelu(lgate)) @ P['w_lru_o']
    y_b = attn @ P['w_mla_o']
    y_c = conformer_branch(conv_in, P)
    ga, gb, gc = jnp.split(jax.nn.sigmoid(gate_logits), N_BRANCH, axis=-1)
    return (ga * y_a + gb * y_b + gc * y_c) @ P['w_out']


def token_mixer(hl, hc, P, cos, sin, ctx_out):
    lx, lg, lq, lkv, lkr, lconv, lgate = split_in(hl @ P['w_in'])
    cx, cg, cq, ckv, ckr, cconv, cgate = split_in(hc @ P['w_in'])
    zeros = jnp.zeros((hc.shape[0], LRU_WIDTH), hc.dtype)
    chf, chb = rglru_bidir(cx, P, zeros, zeros)
    lhf, lhb = rglru_bidir(lx, P, chf[:, -1], chb[:, 0])
    qc, kc, vc = mla_qkv(cq, ckv, ckr, P, None, None)
    ql, kl, vl = mla_qkv(lq, lkv, lkr, P, cos, sin)
    attn_l = attend_blocks(ql, jnp.concatenate([kl, kc], axis=1), jnp.concatenate([vl, vc], axis=1))
    out_l = merge(lhf, lhb, lg, attn_l, lconv, lgate, P)
    if not ctx_out:
        return out_l, None
    attn_c = attend_blocks(qc, kc, vc)
    out_c = merge(chf, chb, cg, attn_c, cconv, cgate, P)
    return out_l, out_c


def ffn(h, P):
    return jnp.square(jax.nn.relu(h @ P['w_ff1'])) @ P['w_ff2']


def setup_inputs(seed: int = 0) -> dict:
    key = jax.random.key(seed)
    ks = iter(jax.random.split(key, 40))

    def nrm(shape, scale):
        return jax.random.normal(next(ks), shape, jnp.float32) * scale

    def gain(shape):
        return 1.0 + nrm(shape, 0.05)

    a0 = jax.random.uniform(next(ks), (DEPTH, 2, LRU_WIDTH), jnp.float32, 0.9, 0.999)
    return {
        'x': nrm((BATCH, SEQ, D_MODEL), 1.0),
        'c': nrm((BATCH, D_MODEL), 1.0),
        'ctx': nrm((BATCH, CTX_LEN, D_MODEL), 1.0),
        'c_ctx': nrm((D_MODEL,), 1.0),
        'w_ada': nrm((DEPTH, D_MODEL, 6 * D_MODEL), 0.5 * D_MODEL ** -0.5),
        'b_ada': nrm((DEPTH, 6 * D_MODEL), 0.02),
        'w_in': nrm((DEPTH, D_MODEL, D_IN), D_MODEL ** -0.5),
        'lru_conv_w': nrm((DEPTH, LRU_CONV, LRU_WIDTH), LRU_CONV ** -0.5),
        'lru_conv_b': nrm((DEPTH, LRU_WIDTH), 0.02),
        'lru_wa': nrm((DEPTH, 2, LRU_HEADS, LRU_BLOCK, LRU_BLOCK), LRU_BLOCK ** -0.5),
        'lru_ba': nrm((DEPTH, 2, LRU_WIDTH), 0.1),
        'lru_wx': nrm((DEPTH, 2, LRU_HEADS, LRU_BLOCK, LRU_BLOCK), LRU_BLOCK ** -0.5),
        'lru_bx': nrm((DEPTH, 2, LRU_WIDTH), 0.1),
        'lru_lambda': jnp.log(a0) - jnp.log1p(-a0),
        'w_lru_o': nrm((DEPTH, LRU_WIDTH, D_MODEL), LRU_WIDTH ** -0.5),
        'mla_q_norm': gain((DEPTH, MLA_Q_RANK)),
        'mla_w_uq': nrm((DEPTH, MLA_Q_RANK, MLA_HEADS * MLA_QK), MLA_Q_RANK ** -0.5),
        'mla_kv_norm': gain((DEPTH, MLA_KV_RANK)),
        'mla_w_ukv': nrm((DEPTH, MLA_KV_RANK, MLA_HEADS * (MLA_NOPE + MLA_V)), MLA_KV_RANK ** -0.5),
        'mla_q_gain': gain((DEPTH, MLA_QK)),
        'mla_k_gain': gain((DEPTH, MLA_QK)),
        'w_mla_o': nrm((DEPTH, MLA_HEADS * MLA_V, D_MODEL), (MLA_HEADS * MLA_V) ** -0.5),
        'conf_dw_w': nrm((DEPTH, CONF_K, CONF_WIDTH), CONF_K ** -0.5),
        'conf_dw_b': nrm((DEPTH, CONF_WIDTH), 0.02),
        'conf_ln_g': gain((DEPTH, CONF_WIDTH)),
        'conf_ln_b': nrm((DEPTH, CONF_WIDTH), 0.02),
        'w_conf_o': nrm((DEPTH, CONF_WIDTH, D_MODEL), CONF_WIDTH ** -0.5),
        'w_out': nrm((DEPTH, D_MODEL, D_MODEL), D_MODEL ** -0.5),
        'w_ff1': nrm((DEPTH, D_MODEL, FFN_HIDDEN), D_MODEL ** -0.5),
        'w_ff2': nrm((DEPTH, FFN_HIDDEN, D_MODEL), FFN_HIDDEN ** -0.5),
    }


def reference(x, c, ctx, c_ctx, w_ada, b_ada, w_in, lru_conv_w, lru_conv_b, lru_wa, lru_ba, lru_wx, lru_bx,
              lru_lambda, w_lru_o, mla_q_norm, mla_w_uq, mla_kv_norm, mla_w_ukv, mla_q_gain, mla_k_gain, w_mla_o,
              conf_dw_w, conf_dw_b, conf_ln_g, conf_ln_b, w_conf_o, w_out, w_ff1, w_ff2):
    cos, sin = axial_rope_tables(x.shape[1])
    s_lat = jax.nn.silu(c)
    s_ctx = jax.nn.silu(c_ctx)
    xl, xc = x, ctx
    for l in range(DEPTH):
        P = dict(w_in=w_in[l], lru_conv_w=lru_conv_w[l], lru_conv_b=lru_conv_b[l], lru_wa=lru_wa[l],
                 lru_ba=lru_ba[l], lru_wx=lru_wx[l], lru_bx=lru_bx[l], lru_lambda=lru_lambda[l],
                 w_lru_o=w_lru_o[l], mla_q_norm=mla_q_norm[l], mla_w_uq=mla_w_uq[l], mla_kv_norm=mla_kv_norm[l],
                 mla_w_ukv=mla_w_ukv[l], mla_q_gain=mla_q_gain[l], mla_k_gain=mla_k_gain[l], w_mla_o=w_mla_o[l],
                 conf_dw_w=conf_dw_w[l], conf_dw_b=conf_dw_b[l], conf_ln_g=conf_ln_g[l], conf_ln_b=conf_ln_b[l],
                 w_conf_o=w_conf_o[l], w_out=w_out[l], w_ff1=w_ff1[l], w_ff2=w_ff2[l])
        ctx_out = l < DEPTH - 1
        sh1, sc1, g1, sh2, sc2, g2 = jnp.split((s_lat @ w_ada[l] + b_ada[l])[:, None, :], 6, axis=-1)
        csh1, csc1, cg1, csh2, csc2, cg2 = jnp.split(s_ctx @ w_ada[l] + b_ada[l], 6, axis=-1)
        hl = modulate(rms_norm(xl), sh1, sc1)
        hc = modulate(rms_norm(xc), csh1, csc1)
        ml, mc = token_mixer(hl, hc, P, cos, sin, ctx_out)
        xl = xl + g1 * ml
        xl = xl + g2 * ffn(modulate(rms_norm(xl), sh2, sc2), P)
        if ctx_out:
            xc = xc + cg1 * mc
            xc = xc + cg2 * ffn(modulate(rms_norm(xc), csh2, csc2), P)
    return xl
```

```python
from contextlib import ExitStack
import numpy as np
import os
import concourse.bass as bass
import concourse.mybir as mybir
from concourse.bass_utils import run_bass_kernel_spmd

F32 = mybir.dt.float32
BF16 = mybir.dt.bfloat16
AF = mybir.ActivationFunctionType
ALU = mybir.AluOpType

D = 1024
CTX = 256
DEPTH_FULL = 4
D_IN = 5792
EPS = 1e-6
NV = 48 + 16 + 4 + 8 + 8 + 8 + 3 + 2 + 4 + 124 + 4 + 4 + 4
V_BADA = 0
V_LCW = 48
V_LCB = 64
V_LBA = 68
V_LBX = 76
V_LAM = 84
V_QN = 92
V_KVN = 95
V_GQ = 97
V_GQP = 98
V_GK = 99
V_GKP = 100
V_CW = 101
V_CB = 225
V_LNG = 229
V_LNB = 233


class _Eng:
    def __init__(self, name, h):
        self.name = name
        self.h = h
        self.sems = []
        self.count = 0
        self.waited = {}
        self.dma_sems = []
        self.dma_vals = []
        self.dma_rr = 0


class MK:
    NSETS = 1

    def __init__(self, nc, n_dma=None):
        self.nc = nc
        self.es = ExitStack()
        self.eng = {}
        n_dma = n_dma or {"sp": 12, "pool": 8}
        for name, h in (("pe", nc.tensor), ("dve", nc.vector), ("act", nc.scalar),
                        ("pool", nc.gpsimd), ("sp", nc.sync)):
            e = _Eng(name, h)
            for s_ in range(self.NSETS):
                e.sems.append(self.es.enter_context(nc.semaphore(f"sem_{name}_{s_}")))
                e.dma_sems.append([self.es.enter_context(nc.semaphore(f"dsem_{name}_{s_}_{i}"))
                                   for i in range(n_dma.get(name, 0))])
                e.dma_vals.append([0] * n_dma.get(name, 0))
            self.eng[name] = e
        self.cur = 0
        self.recs = {}
        self.all_sems = {}
        self.ninst = 0
        self.uid = 0
        self.dummy = self.es.enter_context(nc.sbuf_tensor("mk_dummy", [128, 8], F32))

    def sb(self, st, name, shape, dt):
        self.uid += 1
        return st.enter_context(self.nc.sbuf_tensor(f"{name}_{self.uid}", list(shape), dt))

    def ps(self, st, name, shape, dt=F32):
        self.uid += 1
        return st.enter_context(self.nc.psum_tensor(f"{name}_{self.uid}", list(shape), dt))

    @staticmethod
    def _region(ap):
        a = ap.ap
        pstep = a[0][0]
        off = ap.offset
        p0 = off // pstep
        f = off % pstep
        lo = f
        hi = f
        for st, cnt in a[1:]:
            d = st * (cnt - 1)
            if d < 0:
                lo += d
            else:
                hi += d
        p1 = p0 + a[0][1]
        hi += 1
        if str(ap.space) == "PSUM":
            lo = (lo // 512) * 512
            hi = -(-hi // 512) * 512
            p0 = (p0 // 32) * 32
            p1 = -(-p1 // 32) * 32
        return (p0, p1, lo, hi)

    def _deps_read(self, ap, deps):
        name = ap.name
        r = self._region(ap)
        for rec in self.recs.get(name, ()):
            if rec[4] is not None and rec[0] < r[1] and r[0] < rec[1] and rec[2] < r[3] and r[2] < rec[3]:
                deps.append(rec[4])
        return name, r

    def _deps_write(self, ap, deps):
        name = ap.name
        r = self._region(ap)
        for rec in self.recs.get(name, ()):
            if rec[0] < r[1] and r[0] < rec[1] and rec[2] < r[3] and r[2] < rec[3]:
                if rec[4] is not None:
                    deps.append(rec[4])
                deps.extend(rec[5].values())
        return name, r

    def _note_read(self, name, r, ev):
        lst = self.recs.setdefault(name, [])
        for rec in lst:
            if rec[0] == r[0] and rec[1] == r[1] and rec[2] == r[2] and rec[3] == r[3]:
                rec[5][ev[0]] = ev
                return
        lst.append([r[0], r[1], r[2], r[3], None, {ev[0]: ev}])

    def _note_write(self, name, r, ev):
        lst = self.recs.setdefault(name, [])
        keep = [rec for rec in lst
                if not (r[0] <= rec[0] and rec[1] <= r[1] and r[2] <= rec[2] and rec[3] <= r[3])]
        keep.append([r[0], r[1], r[2], r[3], ev, {}])
        self.recs[name] = keep

    def _wait(self, e, ev, skip_same=False):
        key, sem, val = ev
        if skip_same and key == e.name:
            return
        if e.waited.get(key, 0) >= val:
            return
        e.h.wait_ge(sem, val)
        e.waited[key] = val

    def op(self, engname, build, outs, ins, signal=True):
        e = self.eng[engname]
        deps = []
        rr = [self._deps_read(ap, deps) for ap in ins]
        ww = [self._deps_write(ap, deps) for ap in outs]
        for ev in deps:
            self._wait(e, ev, skip_same=(engname == "pe"))
        inst = build(e.h)
        sem = e.sems[self.cur]
        if signal:
            e.count += 1
            inst.then_inc(sem, 1)
            ev = (e.name, sem, e.count)
            e.pending = False
        else:
            ev = (e.name, sem, e.count + 1)
            e.pending = True
        self.all_sems[e.name] = ev
        for name, r in rr:
            self._note_read(name, r, ev)
        for name, r in ww:
            self._note_write(name, r, ev)
        self.ninst += 1
        return inst

    def dma(self, qname, out, in_, **kw):
        e = self.eng[qname]
        deps = []
        rr = []
        ww = []
        if str(in_.space) != "DRAM":
            rr.append(self._deps_read(in_, deps))
        if str(out.space) != "DRAM":
            ww.append(self._deps_write(out, deps))
        sems = e.dma_sems[self.cur]
        vals = e.dma_vals[self.cur]
        i = e.dma_rr
        e.dma_rr = (i + 1) % len(sems)
        sem = sems[i]
        key = f"d_{qname}_{i}"
        if vals[i] > 0:
            self._wait(e, (key, sem, vals[i]))
        for ev in deps:
            self._wait(e, ev)
        inst = e.h.dma_start(out=out, in_=in_, **kw)
        vals[i] += 16
        inst.then_inc(sem, 16)
        ev = (key, sem, vals[i])
        self.all_sems[key] = ev
        for name, r in rr:
            self._note_read(name, r, ev)
        for name, r in ww:
            self._note_write(name, r, ev)
        self.ninst += 1
        return inst

    def barrier(self):
        assert not getattr(self.eng["pe"], "pending", False)
        for en in ("pe", "dve", "act", "pool", "sp"):
            e = self.eng[en]
            for key, ev in self.all_sems.items():
                if key == en and en == "pe":
                    continue
                self._wait(e, ev)
        self.recs = {}
        if self.NSETS == 1:
            return
        self.all_sems = {}
        self.cur = (self.cur + 1) % self.NSETS
        clr = (self.cur + 1) % self.NSETS
        dve = self.eng["dve"]
        for e in self.eng.values():
            e.count = 0
            e.waited = {}
            dve.h.sem_clear(e.sems[clr])
            for i, sm in enumerate(e.dma_sems[clr]):
                dve.h.sem_clear(sm)
                e.dma_vals[clr][i] = 0
        self.memset("dve", self.dummy[:], 0.0)

    def mm(self, out, lhsT, rhs, start=True, stop=True):
        return self.op("pe", lambda h: h.matmul(out, lhsT, rhs, start=start, stop=stop), [out], [lhsT, rhs],
                       signal=bool(stop))

    def act(self, out, in_, func, bias=None, scale=None):
        ins = [in_]
        kw = {}
        if bias is not None:
            kw["bias"] = bias
            if not isinstance(bias, (int, float)):
                ins.append(bias)
        if scale is not None:
            kw["scale"] = scale
            if not isinstance(scale, (int, float)):
                ins.append(scale)
        return self.op("act", lambda h: h.activation(out, in_, func, **kw), [out], ins)

    def tt(self, eng, out, in0, in1, op):
        return self.op(eng, lambda h: h.tensor_tensor(out, in0, in1, op), [out], [in0, in1])

    def ts(self, eng, out, in0, s1, s2, op0, op1=None):
        ins = [in0] + [s for s in (s1, s2) if s is not None and not isinstance(s, (int, float))]
        if op1 is None:
            return self.op(eng, lambda h: h.tensor_scalar(out, in0, s1, None, op0), [out], ins)
        return self.op(eng, lambda h: h.tensor_scalar(out, in0, s1, s2, op0, op1), [out], ins)

    def stt(self, eng, out, in0, scalar, in1, op0, op1):
        ins = [in0, in1] + ([scalar] if not isinstance(scalar, (int, float)) else [])
        return self.op(eng, lambda h: h.scalar_tensor_tensor(out, in0, scalar, in1, op0, op1), [out], ins)

    def copy(self, eng, out, in_):
        return self.op(eng, lambda h: h.tensor_copy(out, in_), [out], [in_])

    def memset(self, eng, out, val):
        return self.op(eng, lambda h: h.memset(out, val), [out], [])

    def scan(self, eng, out, d0, d1, initial, op0, op1):
        ins = [d0, d1] + ([initial] if not isinstance(initial, (int, float)) else [])
        return self.op(eng, lambda h: h.tensor_tensor_scan(out, d0, d1, initial, op0, op1), [out], ins)

    def recip(self, out, in_):
        return self.op("dve", lambda h: h.reciprocal(out, in_), [out], [in_])


WNAMES = ["w_ada", "w_in", "lru_wa", "lru_wx", "w_lru_o", "mla_w_uq", "mla_w_ukv", "w_mla_o",
          "w_conf_o", "w_out", "w_ff1", "w_ff2"]


def build(S, DEPTH, dbg=()):
    T = S + CTX
    LP = S + 3
    LXP = LP + CTX + 3
    GP = S + 30
    GLP = GP + CTX + 30
    tiles = [(t0, 512, False) for t0 in range(0, S, 512)] + [(S, CTX, True)]
    NKT = T // 128
    nc = bass.Bass("TRN2", target_bir_lowering=False)
    k = MK(nc)

    def din(name, shape, dt=F32):
        return nc.dram_tensor(name, list(shape), dt, kind="ExternalInput").ap()

    def dscr(name, shape, dt):
        kind = "ExternalOutput" if name in dbg else "Internal"
        return nc.dram_tensor(name, list(shape), dt, kind=kind).ap()

    xT_in = din("xT", [D, T])
    cvec_in = din("cvec", [128, 16])
    vecs_in = din("vecs", [DEPTH, 128, NV])
    rope_in = din("rope", [32, 2, T])
    W = {}
    W["w_ada"] = din("w_ada", [DEPTH, D, 6 * D])
    W["w_in"] = din("w_in", [DEPTH, D, D_IN])
    W["lru_wa"] = din("lru_wa", [DEPTH, 2, 8, 64, 64])
    W["lru_wx"] = din("lru_wx", [DEPTH, 2, 8, 64, 64])
    W["w_lru_o"] = din("w_lru_o", [DEPTH, 512, D])
    W["mla_w_uq"] = din("mla_w_uq", [DEPTH, 384, 768])
    W["mla_w_ukv"] = din("mla_w_ukv", [DEPTH, 256, 1024])
    W["w_mla_o"] = din("w_mla_o", [DEPTH, 512, D])
    W["w_conf_o"] = din("w_conf_o", [DEPTH, 512, D])
    W["w_out"] = din("w_out", [DEPTH, D, D])
    W["w_ff1"] = din("w_ff1", [DEPTH, D, 4 * D])
    W["w_ff2"] = din("w_ff2", [DEPTH, 4 * D, D])
    outT = nc.dram_tensor("outT", [D, S], F32, kind="ExternalOutput").ap()

    xa = dscr("xa", [D, T], F32)
    xb = dscr("xb", [D, T], F32)
    lx_s = dscr("lx_s", [512, LXP], BF16)
    glg_s = dscr("glg_s", [512, T], BF16)
    cq_s = dscr("cq_s", [384, T], BF16)
    ckv_s = dscr("ckv_s", [256, T], BF16)
    kr_s = dscr("kr_s", [32, T], BF16)
    glu_s = dscr("glu_s", [512, GLP], BF16)
    gate_s = dscr("gate_s", [3072, T], BF16)
    za_s = dscr("za_s", [512, T], BF16)
    q_s = dscr("q_s", [8, 96, T], BF16)
    k_s = dscr("k_s", [8, 96, T], BF16)
    v_s = dscr("v_s", [T, 520], BF16)
    at_s = dscr("at_s", [512, T], BF16)
    zc_s = dscr("zc_s", [512, T], BF16)

    def fm(ap, p=128):
        return ap.rearrange("(kc p) t -> p kc t", p=p)

    G = ExitStack()
    ones = k.sb(G, "ones", [128, 128], BF16)
    ident = k.sb(G, "ident", [128, 128], BF16)
    identf = k.sb(G, "identf", [128, 128], F32)
    zpad = k.sb(G, "zpad", [128, 4, 32], BF16)
    PB = [k.ps(G, f"pb{i}", [128, 1024]) for i in range(4)]
    PS = [PB[i // 2][:, (i % 2) * 512:(i % 2 + 1) * 512] for i in range(8)]
    k.memset("dve", ones[:], 1.0)
    k.memset("dve", zpad[:], 0.0)
    ident_in = din("ident", [128, 128])
    k.dma("sp", identf[:], ident_in)
    k.copy("dve", ident[:], identf[:])
    k.dma("sp", fm(lx_s)[:, :, 0:2], zpad[:, :, 0:2])
    k.dma("sp", fm(lx_s)[:, :, 2 + S:2 + S + 3], zpad[:, :, 0:3])
    k.dma("sp", fm(lx_s)[:, :, LP + 2 + CTX:LP + 2 + CTX + 1], zpad[:, :, 0:1], allow_slow_non_contiguous=True)
    k.dma("sp", fm(glu_s)[:, :, 0:15], zpad[:, :, 0:15])
    k.dma("sp", fm(glu_s)[:, :, 15 + S:15 + S + 30], zpad[:, :, 0:30])
    k.dma("sp", fm(glu_s)[:, :, GP + 15 + CTX:GP + 15 + CTX + 15], zpad[:, :, 0:15])

    def lxcol(t0, is_ctx):
        return (LP + 2 + (t0 - S)) if is_ctx else (2 + t0)

    def glucol(t0, is_ctx):
        return (GP + 15 + (t0 - S)) if is_ctx else (15 + t0)

    def rstd_from(eng, out, ps_ap, n_feat):
        k.act(out, ps_ap, AF.Ln, bias=EPS, scale=1.0 / n_feat)
        k.act(out, out, AF.Exp, scale=-0.5)

    for l in range(DEPTH):
        last = (l == DEPTH_FULL - 1) if DEPTH == DEPTH_FULL else False
        x_src = xT_in if l == 0 else xa
        act_tiles = tiles if not last else tiles[:-1]

        L = ExitStack()
        vec = k.sb(L, "vec", [128, NV], F32)
        mod = k.sb(L, "mod", [128, 48, 2], F32)
        k.dma("sp", vec[:], vecs_in[l])

        with ExitStack() as P:
            cv = k.sb(P, "cv", [128, 16], F32)
            cvb = k.sb(P, "cvb", [128, 8, 2], BF16)
            wad = [k.sb(P, f"wad{i}", [128, 8, 1024], BF16) for i in range(2)]
            k.dma("sp", cv[:], cvec_in)
            k.act(cvb[:], cv[:].rearrange("p (a b) -> p a b", b=2), AF.Silu)
            mps = PS[0][:, 0:96].rearrange("p (a b) -> p a b", b=2)
            for piece in range(6):
                wt = wad[piece % 2]
                k.dma("pool", wt[:], fm(W["w_ada"][l])[:, :, piece * 1024:(piece + 1) * 1024])
                for j in range(8):
                    for kc in range(8):
                        k.mm(mps[:, piece * 8 + j, :], wt[:, kc, j * 128:(j + 1) * 128], cvb[:, kc, :],
                             start=(kc == 0), stop=(kc == 7))
            for w2 in range(2):
                k.tt("dve", mod[:, :, w2], mps[:, :, w2], vec[:, V_BADA:V_BADA + 48], ALU.add)
            k.ts("dve", mod[:, 8:16, :], mod[:, 8:16, :], 1.0, None, ALU.add)
            k.ts("dve", mod[:, 32:40, :], mod[:, 32:40, :], 1.0, None, ALU.add)
        k.barrier()

        def norm_mod(xt, n, w2, sh_off, sc_off, sqb, rstd, hb, psb):
            k.act(sqb[:, :, 0:n], xt[:, :, 0:n], AF.Square)
            for kc in range(8):
                k.mm(psb[:, 0:n], ones[:], sqb[:, kc, 0:n], start=(kc == 0), stop=(kc == 7))
            rstd_from("dve", rstd[:, 0:n], psb[:, 0:n], D)
            for kc in range(8):
                k.stt("dve", sqb[:, kc, 0:n], xt[:, kc, 0:n], mod[:, sc_off + kc, w2:w2 + 1], rstd[:, 0:n],
                      ALU.mult, ALU.mult)
                k.act(hb[:, kc, 0:n], sqb[:, kc, 0:n], AF.Identity, bias=mod[:, sh_off + kc, w2:w2 + 1], scale=1.0)

        with ExitStack() as P:
            win = k.sb(P, "win", [128, 8, D_IN], BF16)
            for piece in range(8):
                k.dma("pool", win[:, piece, :], W["w_in"][l][piece * 128:(piece + 1) * 128, :])
            xts = [k.sb(P, f"xt{i}", [128, 8, 512], F32) for i in range(2)]
            sqb = k.sb(P, "sqb", [128, 8, 512], BF16)
            rstd = k.sb(P, "rstd", [128, 512], F32)
            hbs = [k.sb(P, f"hb{i}", [128, 8, 512], BF16) for i in range(2)]
            st_lx = k.sb(P, "st_lx", [128, 4, 512], BF16)
            st_lg = k.sb(P, "st_lg", [128, 4, 512], BF16)
            st_cq = k.sb(P, "st_cq", [128, 3, 512], BF16)
            st_ckv = k.sb(P, "st_ckv", [128, 2, 512], BF16)
            st_kr = k.sb(P, "st_kr", [32, 512], BF16)
            st_glu = k.sb(P, "st_glu", [128, 4, 512], BF16)
            st_gate = k.sb(P, "st_gate", [128, 24, 512], BF16)
            g1 = k.sb(P, "g1", [128, 512], F32)
            g2 = k.sb(P, "g2", [128, 512], F32)
            sig = k.sb(P, "sig", [128, 512], F32)
            bank = [0]

            def proj(hb, n, c0, m):
                ps = PS[1 + bank[0] % 6]
                bank[0] += 1
                for kc in range(8):
                    k.mm(ps[0:m, 0:n], win[:, kc, c0:c0 + m], hb[:, kc, 0:n], start=(kc == 0), stop=(kc == 7))
                return ps

            def prep_tile(ti):
                t0_, n_, ctx_ = tiles[ti]
                k.dma("sp", xts[ti % 2][:, :, 0:n_], fm(x_src)[:, :, t0_:t0_ + n_])
                norm_mod(xts[ti % 2], n_, 1 if ctx_ else 0, 0, 8, sqb, rstd, hbs[ti % 2], PS[0])

            prep_tile(0)
            for ti, (t0, n, is_ctx) in enumerate(tiles):
                xt = xts[ti % 2]
                hb = hbs[ti % 2]
                w2 = 1 if is_ctx else 0
                for j in range(4):
                    ps = proj(hb, n, j * 128, 128)
                    k.copy("dve", st_lx[:, j, 0:n], ps[:, 0:n])
                c = lxcol(t0, is_ctx)
                k.dma("sp", fm(lx_s)[:, :, c:c + n], st_lx[:, :, 0:n])
                for j in range(4):
                    ps = proj(hb, n, 512 + j * 128, 128)
                    k.act(g1[:, 0:n], ps[:, 0:n], AF.Square)
                    k.ts("pool", g1[:, 0:n], g1[:, 0:n], 0.044715, 1.0, ALU.mult, ALU.add)
                    k.tt("dve", g2[:, 0:n], g1[:, 0:n], ps[:, 0:n], ALU.mult)
                    k.act(g1[:, 0:n], g2[:, 0:n], AF.Sigmoid, scale=1.5957691216057308)
                    k.tt("dve", st_lg[:, j, 0:n], g1[:, 0:n], ps[:, 0:n], ALU.mult)
                k.dma("sp", fm(glg_s)[:, :, t0:t0 + n], st_lg[:, :, 0:n])
                for j in range(3):
                    ps = proj(hb, n, 1024 + j * 128, 128)
                    k.copy("dve", st_cq[:, j, 0:n], ps[:, 0:n])
                k.dma("sp", fm(cq_s)[:, :, t0:t0 + n], st_cq[:, :, 0:n])
                for j in range(2):
                    ps = proj(hb, n, 1408 + j * 128, 128)
                    k.copy("dve", st_ckv[:, j, 0:n], ps[:, 0:n])
                k.dma("sp", fm(ckv_s)[:, :, t0:t0 + n], st_ckv[:, :, 0:n])
                ps = proj(hb, n, 1664, 32)
                k.copy("dve", st_kr[:, 0:n], ps[0:32, 0:n])
                k.dma("sp", kr_s[:, t0:t0 + n], st_kr[:, 0:n])
                if ti + 1 < len(tiles):
                    prep_tile(ti + 1)
                for j in range(4):
                    pa = proj(hb, n, 1696 + j * 128, 128)
                    pg = proj(hb, n, 2208 + j * 128, 128)
                    k.act(sig[:, 0:n], pg[:, 0:n], AF.Sigmoid)
                    k.tt("dve", st_glu[:, j, 0:n], sig[:, 0:n], pa[:, 0:n], ALU.mult)
                c = glucol(t0, is_ctx)
                k.dma("sp", fm(glu_s)[:, :, c:c + n], st_glu[:, :, 0:n])
                if not (last and is_ctx):
                    for j in range(24):
                        ps = proj(hb, n, 2720 + j * 128, 128)
                        k.act(st_gate[:, j, 0:n], ps[:, 0:n], AF.Sigmoid)
                    k.dma("sp", fm(gate_s)[:, :, t0:t0 + n], st_gate[:, :, 0:n])
        k.barrier()

        with ExitStack() as P:
            wq = k.sb(P, "wq", [128, 3, 768], BF16)
            wqp = k.sb(P, "wqp", [128, 3, 8, 96], BF16)
            wk = k.sb(P, "wk", [128, 2, 8, 64], BF16)
            wv = k.sb(P, "wv", [128, 2, 8, 64], BF16)
            k.dma("pool", wq[:], fm(W["mla_w_uq"][l]))
            k.memset("dve", wqp[:], 0.0)
            uq4 = W["mla_w_uq"][l].rearrange("(kc p) (h d) -> p kc h d", p=128, d=96)
            for kc in range(3):
                k.dma("pool", wqp[:, kc, :, 64:80], uq4[:, kc, :, 80:96])
                k.dma("pool", wqp[:, kc, :, 80:96], uq4[:, kc, :, 64:80])
            ukv4 = W["mla_w_ukv"][l].rearrange("(kc p) (h d) -> p kc h d", p=128, d=128)
            for kc in range(2):
                k.dma("pool", wk[:, kc], ukv4[:, kc, :, 0:64])
                k.dma("pool", wv[:, kc], ukv4[:, kc, :, 64:128])
            cqt = [k.sb(P, f"cqt{i}", [128, 3, 512], BF16) for i in range(2)]
            ckt = [k.sb(P, f"ckt{i}", [128, 2, 512], BF16) for i in range(2)]
            krt = [k.sb(P, f"krt{i}", [128, 2, 512], BF16) for i in range(2)]
            cst = [k.sb(P, f"cst{i}", [128, 2, 512], F32) for i in range(2)]
            sq3 = k.sb(P, "sq3", [128, 3, 512], BF16)
            rs = k.sb(P, "rs", [128, 512], F32)
            cqn = k.sb(P, "cqn", [128, 3, 512], BF16)
            ckn = k.sb(P, "ckn", [128, 2, 512], BF16)
            sqh = [k.sb(P, f"sqh{i}", [128, 512], BF16) for i in range(2)]
            sqk = [k.sb(P, f"sqk{i}", [128, 512], BF16) for i in range(2)]
            rsh = [k.sb(P, f"rsh{i}", [128, 512], F32) for i in range(2)]
            tA = k.sb(P, "tA", [128, 512], F32)
            tB = k.sb(P, "tB", [128, 512], F32)
            krb = k.sb(P, "krb", [128, 512], F32)
            st_q = k.sb(P, "st_q", [96, 8, 512], BF16)
            st_k = k.sb(P, "st_k", [96, 8, 512], BF16)
            vsb = [k.sb(P, f"vsb{i}", [128, 8, 65], BF16) for i in range(2)]
            for i in range(2):
                k.memset("dve", vsb[i][:], 1.0)
            gq = vec[:, V_GQ:V_GQ + 1]
            gqp = vec[:, V_GQP:V_GQP + 1]
            gk = vec[:, V_GK:V_GK + 1]
            gkp = vec[:, V_GKP:V_GKP + 1]
            R = slice(64, 96)
            vcnt = 0
            for ti, (t0, n, is_ctx) in enumerate(tiles):
                cq_t = cqt[ti % 2]
                ck_t = ckt[ti % 2]
                kr_t = krt[ti % 2]
                cs_t = cst[ti % 2]
                k.dma("sp", cq_t[:, :, 0:n], fm(cq_s)[:, :, t0:t0 + n])
                k.dma("sp", ck_t[:, :, 0:n], fm(ckv_s)[:, :, t0:t0 + n])
                k.dma("sp", kr_t[64:96, 0, 0:n], kr_s[:, t0:t0 + n])
                k.dma("sp", kr_t[64:80, 1, 0:n], kr_s[16:32, t0:t0 + n])
                k.dma("sp", kr_t[80:96, 1, 0:n], kr_s[0:16, t0:t0 + n])
                k.dma("sp", cs_t[64:96, :, 0:n], rope_in[:, :, t0:t0 + n])
                k.act(sq3[:, :, 0:n], cq_t[:, :, 0:n], AF.Square)
                for kc in range(3):
                    k.mm(PS[0][:, 0:n], ones[:], sq3[:, kc, 0:n], start=(kc == 0), stop=(kc == 2))
                rstd_from("dve", rs[:, 0:n], PS[0][:, 0:n], 384)
                for kc in range(3):
                    k.stt("dve", cqn[:, kc, 0:n], cq_t[:, kc, 0:n], vec[:, V_QN + kc:V_QN + kc + 1], rs[:, 0:n],
                          ALU.mult, ALU.mult)
                k.act(sq3[:, 0:2, 0:n], ck_t[:, :, 0:n], AF.Square)
                for kc in range(2):
                    k.mm(PS[1][:, 0:n], ones[:], sq3[:, kc, 0:n], start=(kc == 0), stop=(kc == 1))
                rstd_from("dve", rs[:, 0:n], PS[1][:, 0:n], 256)
                for kc in range(2):
                    k.stt("dve", ckn[:, kc, 0:n], ck_t[:, kc, 0:n], vec[:, V_KVN + kc:V_KVN + kc + 1], rs[:, 0:n],
                          ALU.mult, ALU.mult)
                k.stt("dve", krb[R, 0:n], kr_t[R, 0, 0:n], gk[R], cs_t[R, 0, 0:n], ALU.mult, ALU.mult)
                k.stt("dve", tB[R, 0:n], kr_t[R, 1, 0:n], gkp[R], cs_t[R, 1, 0:n], ALU.mult, ALU.mult)
                k.tt("pool", krb[R, 0:n], krb[R, 0:n], tB[R, 0:n], ALU.add)
                for i in range(2):
                    k.act(sqk[i][R, 0:n], kr_t[R, 0, 0:n], AF.Square)
                for h in range(8):
                    qps = PS[2 + (h % 2) * 3]
                    qpp = PS[3 + (h % 2) * 3]
                    ssp = PS[4 + (h % 2) * 3]
                    sq_ = sqh[h % 2]
                    rs_ = rsh[h % 2]
                    for kc in range(3):
                        k.mm(qps[0:96, 0:n], wq[:, kc, h * 96:(h + 1) * 96], cqn[:, kc, 0:n], start=(kc == 0), stop=(kc == 2))
                    for kc in range(3):
                        k.mm(qpp[0:96, 0:n], wqp[:, kc, h, :], cqn[:, kc, 0:n], start=(kc == 0), stop=(kc == 2))
                    k.act(sq_[0:96, 0:n], qps[0:96, 0:n], AF.Square)
                    k.mm(ssp[0:96, 0:n], ones[0:96, 0:96], sq_[0:96, 0:n])
                    rstd_from("dve", rs_[0:96, 0:n], ssp[0:96, 0:n], 96)
                    k.stt("dve", st_q[0:64, h, 0:n], qps[0:64, 0:n], gq[0:64], rs_[0:64, 0:n], ALU.mult, ALU.mult)
                    k.stt("dve", tA[R, 0:n], qps[R, 0:n], gq[R], cs_t[R, 0, 0:n], ALU.mult, ALU.mult)
                    k.stt("dve", tB[R, 0:n], qpp[R, 0:n], gqp[R], cs_t[R, 1, 0:n], ALU.mult, ALU.mult)
                    k.tt("pool", tA[R, 0:n], tA[R, 0:n], tB[R, 0:n], ALU.add)
                    k.tt("pool", st_q[R, h, 0:n], tA[R, 0:n], rs_[R, 0:n], ALU.mult)
                    kps = qpp
                    sk_ = sqk[h % 2]
                    for kc in range(2):
                        k.mm(kps[0:64, 0:n], wk[:, kc, h, :], ckn[:, kc, 0:n], start=(kc == 0), stop=(kc == 1))
                    k.act(sk_[0:64, 0:n], kps[0:64, 0:n], AF.Square)
                    k.mm(ssp[0:96, 0:n], ones[0:96, 0:96], sk_[0:96, 0:n])
                    rstd_from("dve", rs_[0:96, 0:n], ssp[0:96, 0:n], 96)
                    k.stt("dve", st_k[0:64, h, 0:n], kps[0:64, 0:n], gk[0:64], rs_[0:64, 0:n], ALU.mult, ALU.mult)
                    k.tt("pool", st_k[R, h, 0:n], krb[R, 0:n], rs_[R, 0:n], ALU.mult)
                k.dma("sp", q_s.rearrange("h d t -> d h t")[:, :, t0:t0 + n], st_q[:, :, 0:n])
                k.dma("sp", k_s.rearrange("h d t -> d h t")[:, :, t0:t0 + n], st_k[:, :, 0:n])
                for s4 in range(n // 128):
                    vps = PS[0] if s4 % 2 == 0 else PS[1]
                    vt = vsb[vcnt % 2]
                    vcnt += 1
                    for kc in range(2):
                        k.mm(vps[:, 0:512], ckn[:, kc, s4 * 128:(s4 + 1) * 128], wv[:, kc].rearrange("p h d -> p (h d)"),
                             start=(kc == 0), stop=(kc == 1))
                    k.copy("dve", vt[:, :, 0:64], vps[:, 0:512].rearrange("p (h d) -> p h d", d=64))
                    r0 = t0 + s4 * 128
                    k.dma("sp", v_s[r0:r0 + 128, :], vt[:].rearrange("p h d -> p (h d)"))
        k.barrier()

        with ExitStack() as P:
            CH = 2048 if S % 2048 == 0 else S
            lxg = k.sb(P, "lxg", [128, LXP], BF16)
            ub = k.sb(P, "ub", [128, T], BF16)
            hfb = k.sb(P, "hfb", [128, T], BF16)
            tra = k.sb(P, "tra", [128, CH], F32)
            tib = k.sb(P, "tib", [128, CH], F32)
            hd = k.sb(P, "hd", [128, CH], F32)
            zst = k.sb(P, "zst", [128, CH], BF16)
            wbd = k.sb(P, "wbd", [128, 4, 128], BF16)
            negh = k.sb(P, "negh", [128, 8], F32)
            hbias = k.sb(P, "hbias", [128, 16], F32)
            carry = k.sb(P, "carry", [128, 2], F32)
            Bbank = PS[7]
            k.act(negh[:], vec[:, V_LAM:V_LAM + 8], AF.Exp, scale=-1.0)
            k.act(negh[:], negh[:], AF.Ln, bias=1.0, scale=1.0)
            k.ts("dve", negh[:], negh[:], -4.0, None, ALU.mult)
            k.ts("dve", hbias[:], vec[:, V_LBA:V_LBA + 16], 0.5, None, ALU.mult)

            steps = []

            def add(fn):
                steps.append(fn)

            lat_chunks = [(a0, CH) for a0 in range(0, S, CH)]
            for c in range(4):
                def s_load(c=c):
                    k.dma("sp", lxg[:], lx_s[c * 128:(c + 1) * 128, :])
                    k.memset("pool", wbd[:], 0.0)
                    for d in range(2):
                        for ax, nm in enumerate(("lru_wa", "lru_wx")):
                            for hh in range(2):
                                k.dma("pool", wbd[hh * 64:(hh + 1) * 64, d * 2 + ax, hh * 64:(hh + 1) * 64],
                                      W[nm][l, d, 2 * c + hh])
                add(s_load)
                for (o0, n0, src0) in [(a0, nn, a0) for (a0, nn) in lat_chunks] + [(S, CTX, LP)]:
                    def s_conv(c=c, o0=o0, n0=n0, src0=src0):
                        k.ts("dve", hd[:, 0:n0], lxg[:, src0:src0 + n0], vec[:, V_LCW + c * 4:V_LCW + c * 4 + 1],
                             vec[:, V_LCB + c:V_LCB + c + 1], ALU.mult, ALU.add)
                        for j in range(1, 4):
                            k.stt("dve", hd[:, 0:n0], lxg[:, src0 + j:src0 + j + n0],
                                  vec[:, V_LCW + c * 4 + j:V_LCW + c * 4 + j + 1], hd[:, 0:n0], ALU.mult, ALU.add)
                        k.copy("pool", ub[:, o0:o0 + n0], hd[:, 0:n0])
                    add(s_conv)

                def s_glg(c=c):
                    k.dma("sp", lxg[:, 0:T], glg_s[c * 128:(c + 1) * 128, :])
                add(s_glg)
                for d in range(2):
                    order = [(S, CTX)] + (lat_chunks if d == 0 else lat_chunks[::-1])
                    ba_ = hbias[:, d * 4 + c:d * 4 + c + 1]
                    bx_ = hbias[:, 8 + d * 4 + c:8 + d * 4 + c + 1]
                    ng_ = negh[:, d * 4 + c:d * 4 + c + 1]
                    for ci, (a0, nn) in enumerate(order):
                        for s0 in range(0, nn, 256):
                            def s_mm(d=d, a0=a0, s0=s0):
                                k.mm(Bbank[:, 0:256], wbd[:, d * 2 + 0, :], ub[:, a0 + s0:a0 + s0 + 256])
                                k.mm(Bbank[:, 256:512], wbd[:, d * 2 + 1, :], ub[:, a0 + s0:a0 + s0 + 256])
                            add(s_mm)

                            def s_tanh(s0=s0, ba_=ba_, bx_=bx_):
                                k.act(tra[:, s0:s0 + 256], Bbank[:, 0:256], AF.Tanh, bias=ba_, scale=0.5)
                                k.act(tib[:, s0:s0 + 256], Bbank[:, 256:512], AF.Tanh, bias=bx_, scale=0.5)
                            add(s_tanh)

                        def s_e1(nn=nn, a0=a0, ng_=ng_):
                            k.act(tra[:, 0:nn], tra[:, 0:nn], AF.Exp, bias=ng_, scale=ng_)
                            k.stt("dve", tib[:, 0:nn], tib[:, 0:nn], 1.0, ub[:, a0:a0 + nn], ALU.add, ALU.mult)
                        add(s_e1)

                        def s_e2(nn=nn):
                            k.tt("pool", hd[:, 0:nn], tra[:, 0:nn], tra[:, 0:nn], ALU.mult)
                        add(s_e2)

                        def s_e3(nn=nn):
                            k.act(hd[:, 0:nn], hd[:, 0:nn], AF.Sqrt, bias=0.25, scale=-0.25)
                        add(s_e3)

                        def s_e4(nn=nn):
                            k.tt("pool", tib[:, 0:nn], tib[:, 0:nn], hd[:, 0:nn], ALU.mult)
                        add(s_e4)

                        def s_e5(nn=nn, a0=a0, d=d, ci=ci, c=c):
                            init = 0.0 if ci == 0 else carry[:, d:d + 1]
                            if d == 0:
                                k.scan("dve", hd[:, 0:nn], tra[:, 0:nn], tib[:, 0:nn], init, ALU.mult, ALU.add)
                                k.copy("dve", carry[:, 0:1], hd[:, nn - 1:nn])
                                k.copy("pool", hfb[:, a0:a0 + nn], hd[:, 0:nn])
                            else:
                                k.scan("dve", hd[:, 0:nn][:, ::-1], tra[:, 0:nn][:, ::-1], tib[:, 0:nn][:, ::-1], init,
                                       ALU.mult, ALU.add)
                                k.copy("dve", carry[:, 1:2], hd[:, 0:1])
                                k.tt("pool", hd[:, 0:nn], hd[:, 0:nn], hfb[:, a0:a0 + nn], ALU.add)
                                k.tt("pool", zst[:, 0:nn], hd[:, 0:nn], lxg[:, a0:a0 + nn], ALU.mult)
                                k.dma("sp", za_s[c * 128:(c + 1) * 128, a0:a0 + nn], zst[:, 0:nn])
                        add(s_e5)

            bpos = [0]

            def bstep(cnt=1):
                for _ in range(cnt):
                    if bpos[0] >= len(steps):
                        return False
                    steps[bpos[0]]()
                    bpos[0] += 1
                return True

            vhs = [k.sb(P, f"vh{i}", [128, NKT, 65], BF16) for i in range(2)]
            kTs = [k.sb(P, f"kT{i}", [96, T], BF16) for i in range(2)]
            qTs = [k.sb(P, f"qT{i}", [96, T], BF16) for i in range(2)]
            pT = [k.sb(P, f"pT{i}", [128, 2, 512], BF16) for i in range(3)]
            osb = k.sb(P, "osb", [128, 512], F32)
            rden = k.sb(P, "rden", [128, 512], F32)
            bcs = k.sb(P, "bcs", [64, 512], F32)
            onesf = k.sb(P, "onesf", [128, 64], F32)
            st_at = [k.sb(P, f"st_at{i}", [64, 512], BF16) for i in range(2)]
            k.memset("dve", onesf[:], 1.0)
            scale = 96.0 ** -0.5
            qblocks = [(t0, n, is_ctx) for (t0, n, is_ctx) in tiles if not (last and is_ctx)]
            items = []
            for h in range(8):
                for bi, (t0, n, is_ctx) in enumerate(qblocks):
                    kts = list(range(NKT)) if not is_ctx else list(range(S // 128, NKT))
                    for j in range(0, len(kts), 2):
                        items.append((h, bi, t0, n, kts[j], kts[j + 1], j == 0, j == len(kts) - 2))
            if os.environ.get('K_NOATT'):
                items = []
            Sslot = [PB[0], PB[1], PB[2]]
            ops = PS[6]
            slot_ctr = [0]
            LA = 2
            loaded = set()

            def load_head(h):
                if h < 8 and h not in loaded:
                    loaded.add(h)
                    k.dma("sp", kTs[h % 2][:], k_s[h])
                    k.dma("sp", qTs[h % 2][:], q_s[h])
                    k.dma("sp", vhs[h % 2][:],
                          v_s.rearrange("(kt p) (h e) -> p kt h e", p=128, e=65)[:, :, h, :])

            load_head(0)
            nblk = [0]
            if os.environ.get('K_BFIRST'):
                while bstep(1):
                    pass
            for idx in range(len(items) + LA):
                if idx < len(items):
                    h, bi, t0, n, kt0, kt1, first, lastp = items[idx]
                    kT = kTs[h % 2]
                    qT = qTs[h % 2]
                    sl = Sslot[slot_ctr[0] % 3]
                    slot_ctr[0] += 1
                    k.mm(sl[:, 0:n], kT[:, kt0 * 128:(kt0 + 1) * 128], qT[:, t0:t0 + n])
                    k.mm(sl[:, 512:512 + n], kT[:, kt1 * 128:(kt1 + 1) * 128], qT[:, t0:t0 + n])
                    p_ = pT[idx % 3]
                    k.act(p_[:, :, 0:n], sl[:].rearrange("p (a b) -> p a b", b=512)[:, :, 0:n], AF.Exp, scale=scale)
                if idx >= LA:
                    j = idx - LA
                    h, bi, t0, n, kt0, kt1, first, lastp = items[j]
                    if first and bi == 0:
                        load_head(h + 1)
                    vh = vhs[h % 2]
                    p_ = pT[j % 3]
                    k.mm(ops[0:65, 0:n], vh[:, kt0, :], p_[:, 0, 0:n], start=first, stop=False)
                    k.mm(ops[0:65, 0:n], vh[:, kt1, :], p_[:, 1, 0:n], start=False, stop=lastp)
                    if lastp:
                        bno = nblk[0]
                        nblk[0] += 1
                        k.act(osb[0:65, 0:n], ops[0:65, 0:n], AF.Identity)
                        k.recip(rden[64:65, 0:n], osb[64:65, 0:n])
                        bps = Sslot[slot_ctr[0] % 3]
                        slot_ctr[0] += 1
                        k.mm(bps[0:64, 0:n], onesf[64:65, 0:64], rden[64:65, 0:n])
                        k.copy("dve", bcs[:, 0:n], bps[0:64, 0:n])
                        sa = st_at[bno % 2]
                        k.tt("pool", sa[:, 0:n], osb[0:64, 0:n], bcs[:, 0:n], ALU.mult)
                        k.dma("sp", at_s[h * 64:(h + 1) * 64, t0:t0 + n], sa[:, 0:n])
                if not os.environ.get('K_NOB'):
                    target = int(len(steps) * min(1.0, (idx + 1) / max(1.0, 0.9 * len(items)))) if items else len(steps)
                    while bpos[0] < target:
                        bstep(1)
            while (not os.environ.get('K_NOB')) and bstep(1):
                pass
        k.barrier()

        with ExitStack() as P:
            dg = k.sb(P, "dg", [128, 4, 31, 128], BF16)
            for c in range(4):
                for j in range(31):
                    k.ts("pool" if (j % 2) else "dve", dg[:, c, j, :], ident[:],
                         vec[:, V_CW + c * 31 + j:V_CW + c * 31 + j + 1], None, ALU.mult)
            gin = [k.sb(P, f"gin{i}", [128, 4, 542], BF16) for i in range(2)]
            hc = k.sb(P, "hc", [128, 4, 512], F32)
            hcb = k.sb(P, "hcb", [128, 4, 512], BF16)
            hsq = k.sb(P, "hsq", [128, 4, 512], BF16)
            mu = k.sb(P, "mu", [128, 512], F32)
            msq = k.sb(P, "msq", [128, 512], F32)
            rs = k.sb(P, "rsd", [128, 512], F32)
            tmp = k.sb(P, "tmpd", [128, 512], F32)
            st_zc = k.sb(P, "st_zc", [128, 4, 512], BF16)
            for ti, (t0, n, is_ctx) in enumerate(act_tiles):
                g_ = gin[ti % 2]
                c0 = glucol(t0, is_ctx) - 15
                k.dma("sp", g_[:, :, 0:n + 30], fm(glu_s)[:, :, c0:c0 + n + 30])
                for c in range(4):
                    ps = PS[c]
                    for j in range(31):
                        k.mm(ps[:, 0:n], dg[:, c, j, :], g_[:, c, j:j + n], start=(j == 0), stop=(j == 30))
                    k.act(hc[:, c, 0:n], ps[:, 0:n], AF.Identity, bias=vec[:, V_CB + c:V_CB + c + 1], scale=1.0)
                    k.copy("pool", hcb[:, c, 0:n], hc[:, c, 0:n])
                    k.act(hsq[:, c, 0:n], hc[:, c, 0:n], AF.Square)
                for c in range(4):
                    k.mm(PS[4][:, 0:n], ones[:], hcb[:, c, 0:n], start=(c == 0), stop=(c == 3))
                for c in range(4):
                    k.mm(PS[5][:, 0:n], ones[:], hsq[:, c, 0:n], start=(c == 0), stop=(c == 3))
                k.ts("dve", mu[:, 0:n], PS[4][:, 0:n], 1.0 / 512, None, ALU.mult)
                k.tt("dve", msq[:, 0:n], mu[:, 0:n], mu[:, 0:n], ALU.mult)
                k.stt("dve", rs[:, 0:n], PS[5][:, 0:n], 1.0 / 512, msq[:, 0:n], ALU.mult, ALU.subtract)
                k.act(rs[:, 0:n], rs[:, 0:n], AF.Ln, bias=EPS, scale=1.0)
                k.act(rs[:, 0:n], rs[:, 0:n], AF.Exp, scale=-0.5)
                for c in range(4):
                    k.tt("pool", tmp[:, 0:n], hc[:, c, 0:n], mu[:, 0:n], ALU.subtract)
                    k.tt("dve", tmp[:, 0:n], tmp[:, 0:n], rs[:, 0:n], ALU.mult)
                    k.act(st_zc[:, c, 0:n], tmp[:, 0:n], AF.Silu, bias=vec[:, V_LNB + c:V_LNB + c + 1],
                          scale=vec[:, V_LNG + c:V_LNG + c + 1])
                k.dma("sp", fm(zc_s)[:, :, t0:t0 + n], st_zc[:, :, 0:n])
        k.barrier()

        with ExitStack() as P:
            wbo = k.sb(P, "wbo", [128, 3, 4, D], BF16)
            for bi, nm in enumerate(("w_lru_o", "w_mla_o", "w_conf_o")):
                k.dma("pool", wbo[:, bi], fm(W[nm][l]))
            wo = k.sb(P, "wo", [128, 8, D], BF16)
            k.dma("pool", wo[:], fm(W["w_out"][l]))
            zin = [k.sb(P, f"zin{i}", [128, 3, 4, 512], BF16) for i in range(2)]
            gt = [k.sb(P, f"gt{i}", [128, 24, 512], BF16) for i in range(2)]
            xts = [k.sb(P, f"xe{i}", [128, 8, 512], F32) for i in range(2)]
            mm_ = k.sb(P, "mmix", [128, 8, 512], BF16)
            t1s = [k.sb(P, f"t1{i}", [128, 512], F32) for i in range(2)]
            t2s = [k.sb(P, f"t2{i}", [128, 512], F32) for i in range(2)]
            t3s = [k.sb(P, f"t3{i}", [128, 512], F32) for i in range(2)]
            xo = k.sb(P, "xo", [128, 8, 512], F32)
            for ti, (t0, n, is_ctx) in enumerate(act_tiles):
                w2 = 1 if is_ctx else 0
                z_ = zin[ti % 2]
                g_ = gt[ti % 2]
                xt = xts[ti % 2]
                k.dma("sp", z_[:, 0, :, 0:n], fm(za_s)[:, :, t0:t0 + n])
                k.dma("sp", z_[:, 1, :, 0:n], fm(at_s)[:, :, t0:t0 + n])
                k.dma("sp", z_[:, 2, :, 0:n], fm(zc_s)[:, :, t0:t0 + n])
                k.dma("sp", g_[:, :, 0:n], fm(gate_s)[:, :, t0:t0 + n])
                k.dma("sp", xt[:, :, 0:n], fm(x_src)[:, :, t0:t0 + n])
                for m in range(8):
                    pss = [PS[(3 * m + bi) % 6] for bi in range(3)]
                    for bi in range(3):
                        for kc in range(4):
                            k.mm(pss[bi][:, 0:n], wbo[:, bi, kc, m * 128:(m + 1) * 128], z_[:, bi, kc, 0:n],
                                 start=(kc == 0), stop=(kc == 3))
                    t1, t2, t3 = t1s[m % 2], t2s[m % 2], t3s[m % 2]
                    k.tt("dve", t1[:, 0:n], pss[0][:, 0:n], g_[:, m, 0:n], ALU.mult)
                    k.tt("dve", t2[:, 0:n], pss[1][:, 0:n], g_[:, 8 + m, 0:n], ALU.mult)
                    k.tt("dve", t3[:, 0:n], pss[2][:, 0:n], g_[:, 16 + m, 0:n], ALU.mult)
                    k.tt("pool", t1[:, 0:n], t1[:, 0:n], t2[:, 0:n], ALU.add)
                    k.tt("pool", mm_[:, m, 0:n], t1[:, 0:n], t3[:, 0:n], ALU.add)
                for m in range(8):
                    ps = PS[6 + m % 2]
                    for kc in range(8):
                        k.mm(ps[:, 0:n], wo[:, kc, m * 128:(m + 1) * 128], mm_[:, kc, 0:n], start=(kc == 0), stop=(kc == 7))
                    k.stt("dve", xo[:, m, 0:n], ps[:, 0:n], mod[:, 16 + m, w2:w2 + 1], xt[:, m, 0:n], ALU.mult, ALU.add)
                k.dma("sp", fm(xb)[:, :, t0:t0 + n], xo[:, :, 0:n])
        k.barrier()

        with ExitStack() as P:
            w1 = k.sb(P, "w1", [128, 8, 4 * D], BF16)
            w2t = k.sb(P, "w2t", [128, 32, D], BF16)
            for piece in range(8):
                k.dma("pool", w1[:, piece, :], W["w_ff1"][l][piece * 128:(piece + 1) * 128, :])
            for piece in range(8):
                k.dma("pool", w2t[:, piece * 4:(piece + 1) * 4, :],
                      fm(W["w_ff2"][l])[:, piece * 4:(piece + 1) * 4, :])
            xts = [k.sb(P, f"xf{i}", [128, 8, 512], F32) for i in range(2)]
            hff = k.sb(P, "hff", [128, 32, 512], BF16)
            h2 = k.sb(P, "h2", [128, 8, 512], BF16)
            rstd = k.sb(P, "rstd2", [128, 512], F32)
            rl = k.sb(P, "rl", [128, 512], F32)
            for ti, (t0, n, is_ctx) in enumerate(act_tiles):
                w2 = 1 if is_ctx else 0
                xt = xts[ti % 2]
                k.dma("sp", xt[:, :, 0:n], fm(xb)[:, :, t0:t0 + n])
                norm_mod(xt, n, w2, 24, 32, hff[:, 0:8, :], rstd, h2, PS[0])
                for j in range(32):
                    ps = PS[1 + j % 4]
                    for kc in range(8):
                        k.mm(ps[:, 0:n], w1[:, kc, j * 128:(j + 1) * 128], h2[:, kc, 0:n], start=(kc == 0), stop=(kc == 7))
                    k.act(rl[:, 0:n], ps[:, 0:n], AF.Relu)
                    k.tt("pool", hff[:, j, 0:n], rl[:, 0:n], rl[:, 0:n], ALU.mult)
                for m in range(8):
                    ps = PS[5 + m % 3]
                    for kc in range(32):
                        k.mm(ps[:, 0:n], w2t[:, kc, m * 128:(m + 1) * 128], hff[:, kc, 0:n], start=(kc == 0), stop=(kc == 31))
                    k.stt("dve", xt[:, m, 0:n], ps[:, 0:n], mod[:, 40 + m, w2:w2 + 1], xt[:, m, 0:n], ALU.mult, ALU.add)
                if l == DEPTH - 1:
                    if not is_ctx:
                        k.dma("sp", fm(outT)[:, :, t0:t0 + n], xt[:, :, 0:n])
                else:
                    k.dma("sp", fm(xa)[:, :, t0:t0 + n], xt[:, :, 0:n])
        k.barrier()
        L.close()

    k.barrier()
    return nc, k


def _fmcols(v):
    return np.ascontiguousarray(v.reshape(-1, 128).T)


def _pad96(v):
    o = np.zeros((128, 1), np.float32)
    o[:96, 0] = v
    return o


def _partner(g):
    p = g.copy()
    p[64:80] = g[80:96]
    p[80:96] = g[64:80]
    return p


def prep_vecs(inp, depth):
    out = np.zeros((depth, 128, NV), np.float32)
    for l in range(depth):
        cols = []
        cols.append(_fmcols(inp["b_ada"][l]))
        cols.append(np.ascontiguousarray(inp["lru_conv_w"][l].reshape(4, 4, 128).transpose(2, 1, 0)).reshape(128, 16))
        cols.append(_fmcols(inp["lru_conv_b"][l]))
        for nm in ("lru_ba", "lru_bx", "lru_lambda"):
            cols.append(np.ascontiguousarray(inp[nm][l].reshape(2, 4, 128).transpose(2, 0, 1)).reshape(128, 8))
        cols.append(_fmcols(inp["mla_q_norm"][l]))
        cols.append(_fmcols(inp["mla_kv_norm"][l]))
        cols.append(_pad96(inp["mla_q_gain"][l]))
        cols.append(_pad96(_partner(inp["mla_q_gain"][l])))
        cols.append(_pad96(inp["mla_k_gain"][l]))
        cols.append(_pad96(_partner(inp["mla_k_gain"][l])))
        cols.append(np.ascontiguousarray(inp["conf_dw_w"][l].reshape(31, 4, 128).transpose(2, 1, 0)).reshape(128, 124))
        cols.append(_fmcols(inp["conf_dw_b"][l]))
        cols.append(_fmcols(inp["conf_ln_g"][l]))
        cols.append(_fmcols(inp["conf_ln_b"][l]))
        out[l] = np.concatenate(cols, axis=1)
    return out


def rope_table(S):
    T = S + CTX
    rows = S // 64
    row = np.repeat(np.arange(rows, dtype=np.float32), 64)
    col = np.tile(np.arange(64, dtype=np.float32), rows)
    half = 16
    freqs = (np.float32(10000.0) ** (-np.arange(0, half, 2, dtype=np.float32) / np.float32(half))).astype(np.float32)
    ang = np.concatenate([row[:, None] * freqs, col[:, None] * freqs], axis=-1).astype(np.float32)
    cos = np.cos(ang).astype(np.float32).T
    sin = np.sin(ang).astype(np.float32).T
    tab = np.zeros((32, 2, T), np.float32)
    tab[:, 0, S:] = 1.0
    tab[0:16, 0, :S] = cos
    tab[16:32, 0, :S] = cos
    tab[0:16, 1, :S] = -sin
    tab[16:32, 1, :S] = sin
    return tab


def make_in_maps(inp, S, depth, nb):
    vecs = prep_vecs(inp, depth)
    rope = rope_table(S)
    ident = np.eye(128, dtype=np.float32)
    shared = {"vecs": vecs, "rope": rope, "ident": ident}
    for nm in WNAMES:
        shared[nm] = np.ascontiguousarray(inp[nm][:depth])
    maps = []
    for b in range(nb):
        xT = np.ascontiguousarray(np.concatenate([inp["x"][b].T, inp["ctx"][b].T], axis=1))
        cvec = np.ascontiguousarray(np.stack([_fmcols(inp["c"][b]), _fmcols(inp["c_ctx"])], axis=-1).reshape(128, 16))
        m = dict(shared)
        m["xT"] = xT
        m["cvec"] = cvec
        maps.append(m)
    return maps


_CACHE = {}


def kernel(**inputs):
    inp = {k_: np.asarray(v) for k_, v in inputs.items()}
    B, S, _ = inp["x"].shape
    key = (S, DEPTH_FULL)
    if key not in _CACHE:
        _CACHE[key] = build(S, DEPTH_FULL)[0]
    nc = _CACHE[key]
    maps = make_in_maps(inp, S, DEPTH_FULL, B)
    res = run_bass_kernel_spmd(nc, maps, core_ids=list(range(B)))
    out = np.stack([np.ascontiguousarray(r["outT"].T) for r in res.results], axis=0)
    return out.astype(np.float32)
```

```python
from contextlib import ExitStack
import numpy as np
import os
import concourse.bass as bass
import concourse.mybir as mybir
from concourse.bass_utils import run_bass_kernel_spmd

F32 = mybir.dt.float32
BF16 = mybir.dt.bfloat16
AF = mybir.ActivationFunctionType
ALU = mybir.AluOpType

D = 1024
CTX = 256
DEPTH_FULL = 4
D_IN = 5792
EPS = 1e-6
NV = 48 + 16 + 4 + 8 + 8 + 8 + 3 + 2 + 4 + 124 + 4 + 4 + 4
V_BADA = 0
V_LCW = 48
V_LCB = 64
V_LBA = 68
V_LBX = 76
V_LAM = 84
V_QN = 92
V_KVN = 95
V_GQ = 97
V_GQP = 98
V_GK = 99
V_GKP = 100
V_CW = 101
V_CB = 225
V_LNG = 229
V_LNB = 233


class _Eng:
    def __init__(self, name, h):
        self.name = name
        self.h = h
        self.sems = []
        self.count = 0
        self.waited = {}
        self.dma_sems = []
        self.dma_vals = []
        self.dma_rr = 0


class MK:
    NSETS = 1

    def __init__(self, nc, n_dma=None):
        self.nc = nc
        self.es = ExitStack()
        self.eng = {}
        n_dma = n_dma or {"sp": 24, "pool": 8}
        for name, h in (("pe", nc.tensor), ("dve", nc.vector), ("act", nc.scalar),
                        ("pool", nc.gpsimd), ("sp", nc.sync)):
            e = _Eng(name, h)
            for s_ in range(self.NSETS):
                e.sems.append(self.es.enter_context(nc.semaphore(f"sem_{name}_{s_}")))
                e.dma_sems.append([self.es.enter_context(nc.semaphore(f"dsem_{name}_{s_}_{i}"))
                                   for i in range(n_dma.get(name, 0))])
                e.dma_vals.append([0] * n_dma.get(name, 0))
            self.eng[name] = e
        self.cur = 0
        self.recs = {}
        self.all_sems = {}
        self.ninst = 0
        self.uid = 0
        self.dummy = self.es.enter_context(nc.sbuf_tensor("mk_dummy", [128, 8], F32))

    def sb(self, st, name, shape, dt):
        self.uid += 1
        return st.enter_context(self.nc.sbuf_tensor(f"{name}_{self.uid}", list(shape), dt))

    def ps(self, st, name, shape, dt=F32):
        self.uid += 1
        return st.enter_context(self.nc.psum_tensor(f"{name}_{self.uid}", list(shape), dt))

    @staticmethod
    def _region(ap):
        a = ap.ap
        pstep = a[0][0]
        off = ap.offset
        p0 = off // pstep
        f = off % pstep
        lo = f
        hi = f
        for st, cnt in a[1:]:
            d = st * (cnt - 1)
            if d < 0:
                lo += d
            else:
                hi += d
        p1 = p0 + a[0][1]
        hi += 1
        if str(ap.space) == "PSUM":
            lo = (lo // 512) * 512
            hi = -(-hi // 512) * 512
            p0 = (p0 // 32) * 32
            p1 = -(-p1 // 32) * 32
        return (p0, p1, lo, hi)

    def _deps_read(self, ap, deps):
        name = ap.name
        r = self._region(ap)
        for rec in self.recs.get(name, ()):
            if rec[4] is not None and rec[0] < r[1] and r[0] < rec[1] and rec[2] < r[3] and r[2] < rec[3]:
                deps.append(rec[4])
        return name, r

    def _deps_write(self, ap, deps):
        name = ap.name
        r = self._region(ap)
        for rec in self.recs.get(name, ()):
            if rec[0] < r[1] and r[0] < rec[1] and rec[2] < r[3] and r[2] < rec[3]:
                if rec[4] is not None:
                    deps.append(rec[4])
                deps.extend(rec[5].values())
        return name, r

    def _note_read(self, name, r, ev):
        lst = self.recs.setdefault(name, [])
        for rec in lst:
            if rec[0] == r[0] and rec[1] == r[1] and rec[2] == r[2] and rec[3] == r[3]:
                rec[5][ev[0]] = ev
                return
        lst.append([r[0], r[1], r[2], r[3], None, {ev[0]: ev}])

    def _note_write(self, name, r, ev):
        lst = self.recs.setdefault(name, [])
        keep = [rec for rec in lst
                if not (r[0] <= rec[0] and rec[1] <= r[1] and r[2] <= rec[2] and rec[3] <= r[3])]
        keep.append([r[0], r[1], r[2], r[3], ev, {}])
        self.recs[name] = keep

    def _wait(self, e, ev, skip_same=False):
        key, sem, val = ev
        if skip_same and key == e.name:
            return
        if e.waited.get(key, 0) >= val:
            return
        e.h.wait_ge(sem, val)
        e.waited[key] = val

    def op(self, engname, build, outs, ins, signal=True):
        e = self.eng[engname]
        deps = []
        rr = [self._deps_read(ap, deps) for ap in ins]
        ww = [self._deps_write(ap, deps) for ap in outs]
        for ev in deps:
            self._wait(e, ev, skip_same=(engname == "pe"))
        inst = build(e.h)
        sem = e.sems[self.cur]
        if signal:
            e.count += 1
            inst.then_inc(sem, 1)
            ev = (e.name, sem, e.count)
            e.pending = False
        else:
            ev = (e.name, sem, e.count + 1)
            e.pending = True
        self.all_sems[e.name] = ev
        for name, r in rr:
            self._note_read(name, r, ev)
        for name, r in ww:
            self._note_write(name, r, ev)
        self.ninst += 1
        return inst

    def dma(self, qname, out, in_, **kw):
        e = self.eng[qname]
        deps = []
        rr = []
        ww = []
        if str(in_.space) != "DRAM":
            rr.append(self._deps_read(in_, deps))
        if str(out.space) != "DRAM":
            ww.append(self._deps_write(out, deps))
        sems = e.dma_sems[self.cur]
        vals = e.dma_vals[self.cur]
        i = e.dma_rr
        e.dma_rr = (i + 1) % len(sems)
        sem = sems[i]
        key = f"d_{qname}_{i}"
        if vals[i] > 0:
            self._wait(e, (key, sem, vals[i]))
        for ev in deps:
            self._wait(e, ev)
        inst = e.h.dma_start(out=out, in_=in_, **kw)
        vals[i] += 16
        inst.then_inc(sem, 16)
        ev = (key, sem, vals[i])
        self.all_sems[key] = ev
        for name, r in rr:
            self._note_read(name, r, ev)
        for name, r in ww:
            self._note_write(name, r, ev)
        self.ninst += 1
        return inst

    def barrier(self):
        assert not getattr(self.eng["pe"], "pending", False)
        for en in ("pe", "dve", "act", "pool", "sp"):
            e = self.eng[en]
            for key, ev in self.all_sems.items():
                if key == en and en == "pe":
                    continue
                self._wait(e, ev)
        self.recs = {}
        if self.NSETS == 1:
            return
        self.all_sems = {}
        self.cur = (self.cur + 1) % self.NSETS
        clr = (self.cur + 1) % self.NSETS
        dve = self.eng["dve"]
        for e in self.eng.values():
            e.count = 0
            e.waited = {}
            dve.h.sem_clear(e.sems[clr])
            for i, sm in enumerate(e.dma_sems[clr]):
                dve.h.sem_clear(sm)
                e.dma_vals[clr][i] = 0
        self.memset("dve", self.dummy[:], 0.0)

    def mm(self, out, lhsT, rhs, start=True, stop=True):
        return self.op("pe", lambda h: h.matmul(out, lhsT, rhs, start=start, stop=stop), [out], [lhsT, rhs],
                       signal=bool(stop))

    def act(self, out, in_, func, bias=None, scale=None):
        ins = [in_]
        kw = {}
        if bias is not None:
            kw["bias"] = bias
            if not isinstance(bias, (int, float)):
                ins.append(bias)
        if scale is not None:
            kw["scale"] = scale
            if not isinstance(scale, (int, float)):
                ins.append(scale)
        return self.op("act", lambda h: h.activation(out, in_, func, **kw), [out], ins)

    def tt(self, eng, out, in0, in1, op):
        return self.op(eng, lambda h: h.tensor_tensor(out, in0, in1, op), [out], [in0, in1])

    def ts(self, eng, out, in0, s1, s2, op0, op1=None):
        ins = [in0] + [s for s in (s1, s2) if s is not None and not isinstance(s, (int, float))]
        if op1 is None:
            return self.op(eng, lambda h: h.tensor_scalar(out, in0, s1, None, op0), [out], ins)
        return self.op(eng, lambda h: h.tensor_scalar(out, in0, s1, s2, op0, op1), [out], ins)

    def stt(self, eng, out, in0, scalar, in1, op0, op1):
        ins = [in0, in1] + ([scalar] if not isinstance(scalar, (int, float)) else [])
        return self.op(eng, lambda h: h.scalar_tensor_tensor(out, in0, scalar, in1, op0, op1), [out], ins)

    def copy(self, eng, out, in_):
        return self.op(eng, lambda h: h.tensor_copy(out, in_), [out], [in_])

    def memset(self, eng, out, val):
        return self.op(eng, lambda h: h.memset(out, val), [out], [])

    def scan(self, eng, out, d0, d1, initial, op0, op1):
        ins = [d0, d1] + ([initial] if not isinstance(initial, (int, float)) else [])
        return self.op(eng, lambda h: h.tensor_tensor_scan(out, d0, d1, initial, op0, op1), [out], ins)

    def recip(self, out, in_):
        return self.op("dve", lambda h: h.reciprocal(out, in_), [out], [in_])


WNAMES = ["w_ada", "w_in", "lru_wa", "lru_wx", "w_lru_o", "mla_w_uq", "mla_w_ukv", "w_mla_o",
          "w_conf_o", "w_out", "w_ff1", "w_ff2"]


def build(S, DEPTH, dbg=()):
    T = S + CTX
    LP = S + 3
    LXP = LP + CTX + 3
    GP = S + 30
    GLP = GP + CTX + 30
    tiles = [(t0, 512, False) for t0 in range(0, S, 512)] + [(S, CTX, True)]
    NKT = T // 128
    nc = bass.Bass("TRN2", target_bir_lowering=False)
    k = MK(nc)

    def din(name, shape, dt=F32):
        return nc.dram_tensor(name, list(shape), dt, kind="ExternalInput").ap()

    def dscr(name, shape, dt):
        kind = "ExternalOutput" if name in dbg else "Internal"
        return nc.dram_tensor(name, list(shape), dt, kind=kind).ap()

    xT_in = din("xT", [D, T])
    cvec_in = din("cvec", [128, 16])
    vecs_in = din("vecs", [DEPTH, 128, NV])
    rope_in = din("rope", [32, 2, T])
    W = {}
    W["w_ada"] = din("w_ada", [DEPTH, D, 6 * D])
    W["w_in"] = din("w_in", [DEPTH, D, D_IN])
    W["lru_wa"] = din("lru_wa", [DEPTH, 2, 8, 64, 64])
    W["lru_wx"] = din("lru_wx", [DEPTH, 2, 8, 64, 64])
    W["w_lru_o"] = din("w_lru_o", [DEPTH, 512, D])
    W["mla_w_uq"] = din("mla_w_uq", [DEPTH, 384, 768])
    W["mla_w_ukv"] = din("mla_w_ukv", [DEPTH, 256, 1024])
    W["w_mla_o"] = din("w_mla_o", [DEPTH, 512, D])
    W["w_conf_o"] = din("w_conf_o", [DEPTH, 512, D])
    W["w_out"] = din("w_out", [DEPTH, D, D])
    W["w_ff1"] = din("w_ff1", [DEPTH, D, 4 * D])
    W["w_ff2"] = din("w_ff2", [DEPTH, 4 * D, D])
    outT = nc.dram_tensor("outT", [D, S], F32, kind="ExternalOutput").ap()

    xa = dscr("xa", [D, T], F32)
    xb = dscr("xb", [D, T], F32)
    lx_s = dscr("lx_s", [512, LXP], BF16)
    glg_s = dscr("glg_s", [512, T], BF16)
    cq_s = dscr("cq_s", [384, T], BF16)
    ckv_s = dscr("ckv_s", [256, T], BF16)
    kr_s = dscr("kr_s", [32, T], BF16)
    glu_s = dscr("glu_s", [512, GLP], BF16)
    gate_s = dscr("gate_s", [3072, T], BF16)
    za_s = dscr("za_s", [512, T], BF16)
    q_s = dscr("q_s", [8, 96, T], BF16)
    k_s = dscr("k_s", [8, 96, T], BF16)
    v_s = dscr("v_s", [T, 520], BF16)
    at_s = dscr("at_s", [512, T], BF16)
    zc_s = dscr("zc_s", [512, T], BF16)

    def fm(ap, p=128):
        return ap.rearrange("(kc p) t -> p kc t", p=p)

    G = ExitStack()
    ones = k.sb(G, "ones", [128, 128], BF16)
    ident = k.sb(G, "ident", [128, 128], BF16)
    identf = k.sb(G, "identf", [128, 128], F32)
    zpad = k.sb(G, "zpad", [128, 4, 32], BF16)
    PB = [k.ps(G, f"pb{i}", [128, 1024]) for i in range(4)]
    PS = [PB[i // 2][:, (i % 2) * 512:(i % 2 + 1) * 512] for i in range(8)]
    k.memset("dve", ones[:], 1.0)
    k.memset("dve", zpad[:], 0.0)
    ident_in = din("ident", [128, 128])
    k.dma("sp", identf[:], ident_in)
    k.copy("dve", ident[:], identf[:])
    k.dma("sp", fm(lx_s)[:, :, 0:2], zpad[:, :, 0:2])
    k.dma("sp", fm(lx_s)[:, :, 2 + S:2 + S + 3], zpad[:, :, 0:3])
    k.dma("sp", fm(lx_s)[:, :, LP + 2 + CTX:LP + 2 + CTX + 1], zpad[:, :, 0:1], allow_slow_non_contiguous=True)
    k.dma("sp", fm(glu_s)[:, :, 0:15], zpad[:, :, 0:15])
    k.dma("sp", fm(glu_s)[:, :, 15 + S:15 + S + 30], zpad[:, :, 0:30])
    k.dma("sp", fm(glu_s)[:, :, GP + 15 + CTX:GP + 15 + CTX + 15], zpad[:, :, 0:15])

    def lxcol(t0, is_ctx):
        return (LP + 2 + (t0 - S)) if is_ctx else (2 + t0)

    def glucol(t0, is_ctx):
        return (GP + 15 + (t0 - S)) if is_ctx else (15 + t0)

    def rstd_from(eng, out, ps_ap, n_feat):
        k.act(out, ps_ap, AF.Ln, bias=EPS, scale=1.0 / n_feat)
        k.act(out, out, AF.Exp, scale=-0.5)

    for l in range(DEPTH):
        last = (l == DEPTH_FULL - 1) if DEPTH == DEPTH_FULL else False
        x_src = xT_in if l == 0 else xa
        act_tiles = tiles if not last else tiles[:-1]

        L = ExitStack()
        vec = k.sb(L, "vec", [128, NV], F32)
        mod = k.sb(L, "mod", [128, 48, 2], F32)
        k.dma("sp", vec[:], vecs_in[l])

        with ExitStack() as P:
            cv = k.sb(P, "cv", [128, 16], F32)
            cvb = k.sb(P, "cvb", [128, 8, 2], BF16)
            wad = [k.sb(P, f"wad{i}", [128, 8, 1024], BF16) for i in range(2)]
            k.dma("sp", cv[:], cvec_in)
            k.act(cvb[:], cv[:].rearrange("p (a b) -> p a b", b=2), AF.Silu)
            mps = PS[0][:, 0:96].rearrange("p (a b) -> p a b", b=2)
            for piece in range(6):
                wt = wad[piece % 2]
                k.dma("pool", wt[:], fm(W["w_ada"][l])[:, :, piece * 1024:(piece + 1) * 1024])
                for j in range(8):
                    for kc in range(8):
                        k.mm(mps[:, piece * 8 + j, :], wt[:, kc, j * 128:(j + 1) * 128], cvb[:, kc, :],
                             start=(kc == 0), stop=(kc == 7))
            for w2 in range(2):
                k.tt("dve", mod[:, :, w2], mps[:, :, w2], vec[:, V_BADA:V_BADA + 48], ALU.add)
            k.ts("dve", mod[:, 8:16, :], mod[:, 8:16, :], 1.0, None, ALU.add)
            k.ts("dve", mod[:, 32:40, :], mod[:, 32:40, :], 1.0, None, ALU.add)
        k.barrier()

        def norm_mod(xt, n, w2, sh_off, sc_off, sqb, rstd, hb, psb):
            k.act(sqb[:, :, 0:n], xt[:, :, 0:n], AF.Square)
            for kc in range(8):
                k.mm(psb[:, 0:n], ones[:], sqb[:, kc, 0:n], start=(kc == 0), stop=(kc == 7))
            rstd_from("dve", rstd[:, 0:n], psb[:, 0:n], D)
            for kc in range(8):
                k.stt("dve", sqb[:, kc, 0:n], xt[:, kc, 0:n], mod[:, sc_off + kc, w2:w2 + 1], rstd[:, 0:n],
                      ALU.mult, ALU.mult)
                k.act(hb[:, kc, 0:n], sqb[:, kc, 0:n], AF.Identity, bias=mod[:, sh_off + kc, w2:w2 + 1], scale=1.0)

        with ExitStack() as P:
            win = k.sb(P, "win", [128, 8, D_IN], BF16)
            for piece in range(8):
                k.dma("pool", win[:, piece, :], W["w_in"][l][piece * 128:(piece + 1) * 128, :])
            xts = [k.sb(P, f"xt{i}", [128, 8, 512], F32) for i in range(2)]
            sqb = k.sb(P, "sqb", [128, 8, 512], BF16)
            rstd = k.sb(P, "rstd", [128, 512], F32)
            hbs = [k.sb(P, f"hb{i}", [128, 8, 512], BF16) for i in range(2)]
            st_lx = k.sb(P, "st_lx", [128, 4, 512], BF16)
            st_lg = k.sb(P, "st_lg", [128, 4, 512], BF16)
            st_cq = k.sb(P, "st_cq", [128, 3, 512], BF16)
            st_ckv = k.sb(P, "st_ckv", [128, 2, 512], BF16)
            st_kr = k.sb(P, "st_kr", [32, 512], BF16)
            st_glu = k.sb(P, "st_glu", [128, 4, 512], BF16)
            st_gate = k.sb(P, "st_gate", [128, 24, 512], BF16)
            g1 = k.sb(P, "g1", [128, 512], F32)
            g2 = k.sb(P, "g2", [128, 512], F32)
            sig = k.sb(P, "sig", [128, 512], F32)
            bank = [0]

            def proj(hb, n, c0, m):
                ps = PS[1 + bank[0] % 6]
                bank[0] += 1
                for kc in range(8):
                    k.mm(ps[0:m, 0:n], win[:, kc, c0:c0 + m], hb[:, kc, 0:n], start=(kc == 0), stop=(kc == 7))
                return ps

            def prep_tile(ti):
                t0_, n_, ctx_ = tiles[ti]
                k.dma("sp", xts[ti % 2][:, :, 0:n_], fm(x_src)[:, :, t0_:t0_ + n_])
                norm_mod(xts[ti % 2], n_, 1 if ctx_ else 0, 0, 8, sqb, rstd, hbs[ti % 2], PS[0])

            prep_tile(0)
            for ti, (t0, n, is_ctx) in enumerate(tiles):
                xt = xts[ti % 2]
                hb = hbs[ti % 2]
                w2 = 1 if is_ctx else 0
                for j in range(4):
                    ps = proj(hb, n, j * 128, 128)
                    k.copy("dve", st_lx[:, j, 0:n], ps[:, 0:n])
                c = lxcol(t0, is_ctx)
                k.dma("sp", fm(lx_s)[:, :, c:c + n], st_lx[:, :, 0:n])
                for j in range(4):
                    ps = proj(hb, n, 512 + j * 128, 128)
                    k.act(g1[:, 0:n], ps[:, 0:n], AF.Square)
                    k.ts("pool", g1[:, 0:n], g1[:, 0:n], 0.044715, 1.0, ALU.mult, ALU.add)
                    k.tt("dve", g2[:, 0:n], g1[:, 0:n], ps[:, 0:n], ALU.mult)
                    k.act(g1[:, 0:n], g2[:, 0:n], AF.Sigmoid, scale=1.5957691216057308)
                    k.tt("dve", st_lg[:, j, 0:n], g1[:, 0:n], ps[:, 0:n], ALU.mult)
                k.dma("sp", fm(glg_s)[:, :, t0:t0 + n], st_lg[:, :, 0:n])
                for j in range(3):
                    ps = proj(hb, n, 1024 + j * 128, 128)
                    k.copy("dve", st_cq[:, j, 0:n], ps[:, 0:n])
                k.dma("sp", fm(cq_s)[:, :, t0:t0 + n], st_cq[:, :, 0:n])
                for j in range(2):
                    ps = proj(hb, n, 1408 + j * 128, 128)
                    k.copy("dve", st_ckv[:, j, 0:n], ps[:, 0:n])
                k.dma("sp", fm(ckv_s)[:, :, t0:t0 + n], st_ckv[:, :, 0:n])
                ps = proj(hb, n, 1664, 32)
                k.copy("dve", st_kr[:, 0:n], ps[0:32, 0:n])
                k.dma("sp", kr_s[:, t0:t0 + n], st_kr[:, 0:n])
                if ti + 1 < len(tiles):
                    prep_tile(ti + 1)
                for j in range(4):
                    pa = proj(hb, n, 1696 + j * 128, 128)
                    pg = proj(hb, n, 2208 + j * 128, 128)
                    k.act(sig[:, 0:n], pg[:, 0:n], AF.Sigmoid)
                    k.tt("dve", st_glu[:, j, 0:n], sig[:, 0:n], pa[:, 0:n], ALU.mult)
                c = glucol(t0, is_ctx)
                k.dma("sp", fm(glu_s)[:, :, c:c + n], st_glu[:, :, 0:n])
                if not (last and is_ctx):
                    for j in range(24):
                        ps = proj(hb, n, 2720 + j * 128, 128)
                        k.act(st_gate[:, j, 0:n], ps[:, 0:n], AF.Sigmoid)
                    k.dma("sp", fm(gate_s)[:, :, t0:t0 + n], st_gate[:, :, 0:n])
        k.barrier()

        with ExitStack() as P:
            wq = k.sb(P, "wq", [128, 3, 768], BF16)
            wqp = k.sb(P, "wqp", [128, 3, 8, 96], BF16)
            wk = k.sb(P, "wk", [128, 2, 8, 64], BF16)
            wv = k.sb(P, "wv", [128, 2, 8, 64], BF16)
            k.dma("pool", wq[:], fm(W["mla_w_uq"][l]))
            k.memset("dve", wqp[:], 0.0)
            uq4 = W["mla_w_uq"][l].rearrange("(kc p) (h d) -> p kc h d", p=128, d=96)
            for kc in range(3):
                k.dma("pool", wqp[:, kc, :, 64:80], uq4[:, kc, :, 80:96])
                k.dma("pool", wqp[:, kc, :, 80:96], uq4[:, kc, :, 64:80])
            ukv4 = W["mla_w_ukv"][l].rearrange("(kc p) (h d) -> p kc h d", p=128, d=128)
            for kc in range(2):
                k.dma("pool", wk[:, kc], ukv4[:, kc, :, 0:64])
                k.dma("pool", wv[:, kc], ukv4[:, kc, :, 64:128])
            cqt = [k.sb(P, f"cqt{i}", [128, 3, 512], BF16) for i in range(2)]
            ckt = [k.sb(P, f"ckt{i}", [128, 2, 512], BF16) for i in range(2)]
            krt = [k.sb(P, f"krt{i}", [128, 2, 512], BF16) for i in range(2)]
            cst = [k.sb(P, f"cst{i}", [128, 2, 512], F32) for i in range(2)]
            sq3 = k.sb(P, "sq3", [128, 3, 512], BF16)
            rs = k.sb(P, "rs", [128, 512], F32)
            cqn = k.sb(P, "cqn", [128, 3, 512], BF16)
            ckn = k.sb(P, "ckn", [128, 2, 512], BF16)
            sqh = [k.sb(P, f"sqh{i}", [128, 512], BF16) for i in range(2)]
            sqk = [k.sb(P, f"sqk{i}", [128, 512], BF16) for i in range(2)]
            rsh = [k.sb(P, f"rsh{i}", [128, 512], F32) for i in range(2)]
            tA = k.sb(P, "tA", [128, 512], F32)
            tB = k.sb(P, "tB", [128, 512], F32)
            krb = k.sb(P, "krb", [128, 512], F32)
            st_q = k.sb(P, "st_q", [96, 8, 512], BF16)
            st_k = k.sb(P, "st_k", [96, 8, 512], BF16)
            vsb = [k.sb(P, f"vsb{i}", [128, 8, 65], BF16) for i in range(2)]
            for i in range(2):
                k.memset("dve", vsb[i][:], 1.0)
            gq = vec[:, V_GQ:V_GQ + 1]
            gqp = vec[:, V_GQP:V_GQP + 1]
            gk = vec[:, V_GK:V_GK + 1]
            gkp = vec[:, V_GKP:V_GKP + 1]
            R = slice(64, 96)
            vcnt = 0
            for ti, (t0, n, is_ctx) in enumerate(tiles):
                cq_t = cqt[ti % 2]
                ck_t = ckt[ti % 2]
                kr_t = krt[ti % 2]
                cs_t = cst[ti % 2]
                k.dma("sp", cq_t[:, :, 0:n], fm(cq_s)[:, :, t0:t0 + n])
                k.dma("sp", ck_t[:, :, 0:n], fm(ckv_s)[:, :, t0:t0 + n])
                k.dma("sp", kr_t[64:96, 0, 0:n], kr_s[:, t0:t0 + n])
                k.dma("sp", kr_t[64:80, 1, 0:n], kr_s[16:32, t0:t0 + n])
                k.dma("sp", kr_t[80:96, 1, 0:n], kr_s[0:16, t0:t0 + n])
                k.dma("sp", cs_t[64:96, :, 0:n], rope_in[:, :, t0:t0 + n])
                k.act(sq3[:, :, 0:n], cq_t[:, :, 0:n], AF.Square)
                for kc in range(3):
                    k.mm(PS[0][:, 0:n], ones[:], sq3[:, kc, 0:n], start=(kc == 0), stop=(kc == 2))
                rstd_from("dve", rs[:, 0:n], PS[0][:, 0:n], 384)
                for kc in range(3):
                    k.stt("dve", cqn[:, kc, 0:n], cq_t[:, kc, 0:n], vec[:, V_QN + kc:V_QN + kc + 1], rs[:, 0:n],
                          ALU.mult, ALU.mult)
                k.act(sq3[:, 0:2, 0:n], ck_t[:, :, 0:n], AF.Square)
                for kc in range(2):
                    k.mm(PS[1][:, 0:n], ones[:], sq3[:, kc, 0:n], start=(kc == 0), stop=(kc == 1))
                rstd_from("dve", rs[:, 0:n], PS[1][:, 0:n], 256)
                for kc in range(2):
                    k.stt("dve", ckn[:, kc, 0:n], ck_t[:, kc, 0:n], vec[:, V_KVN + kc:V_KVN + kc + 1], rs[:, 0:n],
                          ALU.mult, ALU.mult)
                k.stt("dve", krb[R, 0:n], kr_t[R, 0, 0:n], gk[R], cs_t[R, 0, 0:n], ALU.mult, ALU.mult)
                k.stt("dve", tB[R, 0:n], kr_t[R, 1, 0:n], gkp[R], cs_t[R, 1, 0:n], ALU.mult, ALU.mult)
                k.tt("pool", krb[R, 0:n], krb[R, 0:n], tB[R, 0:n], ALU.add)
                for i in range(2):
                    k.act(sqk[i][R, 0:n], kr_t[R, 0, 0:n], AF.Square)
                for h in range(8):
                    qps = PS[2 + (h % 2) * 3]
                    qpp = PS[3 + (h % 2) * 3]
                    ssp = PS[4 + (h % 2) * 3]
                    sq_ = sqh[h % 2]
                    rs_ = rsh[h % 2]
                    for kc in range(3):
                        k.mm(qps[0:96, 0:n], wq[:, kc, h * 96:(h + 1) * 96], cqn[:, kc, 0:n], start=(kc == 0), stop=(kc == 2))
                    for kc in range(3):
                        k.mm(qpp[0:96, 0:n], wqp[:, kc, h, :], cqn[:, kc, 0:n], start=(kc == 0), stop=(kc == 2))
                    k.act(sq_[0:96, 0:n], qps[0:96, 0:n], AF.Square)
                    k.mm(ssp[0:96, 0:n], ones[0:96, 0:96], sq_[0:96, 0:n])
                    rstd_from("dve", rs_[0:96, 0:n], ssp[0:96, 0:n], 96)
                    k.stt("dve", st_q[0:64, h, 0:n], qps[0:64, 0:n], gq[0:64], rs_[0:64, 0:n], ALU.mult, ALU.mult)
                    k.stt("dve", tA[R, 0:n], qps[R, 0:n], gq[R], cs_t[R, 0, 0:n], ALU.mult, ALU.mult)
                    k.stt("dve", tB[R, 0:n], qpp[R, 0:n], gqp[R], cs_t[R, 1, 0:n], ALU.mult, ALU.mult)
                    k.tt("pool", tA[R, 0:n], tA[R, 0:n], tB[R, 0:n], ALU.add)
                    k.tt("pool", st_q[R, h, 0:n], tA[R, 0:n], rs_[R, 0:n], ALU.mult)
                    kps = qpp
                    sk_ = sqk[h % 2]
                    for kc in range(2):
                        k.mm(kps[0:64, 0:n], wk[:, kc, h, :], ckn[:, kc, 0:n], start=(kc == 0), stop=(kc == 1))
                    k.act(sk_[0:64, 0:n], kps[0:64, 0:n], AF.Square)
                    k.mm(ssp[0:96, 0:n], ones[0:96, 0:96], sk_[0:96, 0:n])
                    rstd_from("dve", rs_[0:96, 0:n], ssp[0:96, 0:n], 96)
                    k.stt("dve", st_k[0:64, h, 0:n], kps[0:64, 0:n], gk[0:64], rs_[0:64, 0:n], ALU.mult, ALU.mult)
                    k.tt("pool", st_k[R, h, 0:n], krb[R, 0:n], rs_[R, 0:n], ALU.mult)
                k.dma("sp", q_s.rearrange("h d t -> d h t")[:, :, t0:t0 + n], st_q[:, :, 0:n])
                k.dma("sp", k_s.rearrange("h d t -> d h t")[:, :, t0:t0 + n], st_k[:, :, 0:n])
                for s4 in range(n // 128):
                    vps = PS[0] if s4 % 2 == 0 else PS[1]
                    vt = vsb[vcnt % 2]
                    vcnt += 1
                    for kc in range(2):
                        k.mm(vps[:, 0:512], ckn[:, kc, s4 * 128:(s4 + 1) * 128], wv[:, kc].rearrange("p h d -> p (h d)"),
                             start=(kc == 0), stop=(kc == 1))
                    k.copy("dve", vt[:, :, 0:64], vps[:, 0:512].rearrange("p (h d) -> p h d", d=64))
                    r0 = t0 + s4 * 128
                    k.dma("sp", v_s[r0:r0 + 128, :], vt[:].rearrange("p h d -> p (h d)"))
        k.barrier()

        with ExitStack() as P:
            CH = 2048 if S % 2048 == 0 else S
            lxg = k.sb(P, "lxg", [128, LXP], BF16)
            ub = k.sb(P, "ub", [128, T], BF16)
            hfb = k.sb(P, "hfb", [128, T], BF16)
            tra = k.sb(P, "tra", [128, CH], F32)
            tib = k.sb(P, "tib", [128, CH], F32)
            hd = k.sb(P, "hd", [128, CH], F32)
            zst = k.sb(P, "zst", [128, CH], BF16)
            wbd = k.sb(P, "wbd", [128, 4, 128], BF16)
            negh = k.sb(P, "negh", [128, 8], F32)
            hbias = k.sb(P, "hbias", [128, 16], F32)
            carry = k.sb(P, "carry", [128, 2], F32)
            Bbank = PS[7]
            k.act(negh[:], vec[:, V_LAM:V_LAM + 8], AF.Exp, scale=-1.0)
            k.act(negh[:], negh[:], AF.Ln, bias=1.0, scale=1.0)
            k.ts("dve", negh[:], negh[:], -4.0, None, ALU.mult)
            k.ts("dve", hbias[:], vec[:, V_LBA:V_LBA + 16], 0.5, None, ALU.mult)

            steps = []

            def add(fn):
                steps.append(fn)

            lat_chunks = [(a0, CH) for a0 in range(0, S, CH)]
            for c in range(4):
                def s_load(c=c):
                    k.dma("sp", lxg[:], lx_s[c * 128:(c + 1) * 128, :])
                    k.memset("pool", wbd[:], 0.0)
                    for d in range(2):
                        for ax, nm in enumerate(("lru_wa", "lru_wx")):
                            for hh in range(2):
                                k.dma("pool", wbd[hh * 64:(hh + 1) * 64, d * 2 + ax, hh * 64:(hh + 1) * 64],
                                      W[nm][l, d, 2 * c + hh])
                add(s_load)
                for (o0, n0, src0) in [(a0, nn, a0) for (a0, nn) in lat_chunks] + [(S, CTX, LP)]:
                    def s_conv(c=c, o0=o0, n0=n0, src0=src0):
                        k.ts("dve", hd[:, 0:n0], lxg[:, src0:src0 + n0], vec[:, V_LCW + c * 4:V_LCW + c * 4 + 1],
                             vec[:, V_LCB + c:V_LCB + c + 1], ALU.mult, ALU.add)
                        for j in range(1, 4):
                            k.stt("dve", hd[:, 0:n0], lxg[:, src0 + j:src0 + j + n0],
                                  vec[:, V_LCW + c * 4 + j:V_LCW + c * 4 + j + 1], hd[:, 0:n0], ALU.mult, ALU.add)
                        k.copy("pool", ub[:, o0:o0 + n0], hd[:, 0:n0])
                    add(s_conv)

                def s_glg(c=c):
                    k.dma("sp", lxg[:, 0:T], glg_s[c * 128:(c + 1) * 128, :])
                add(s_glg)
                for d in range(2):
                    order = [(S, CTX)] + (lat_chunks if d == 0 else lat_chunks[::-1])
                    ba_ = hbias[:, d * 4 + c:d * 4 + c + 1]
                    bx_ = hbias[:, 8 + d * 4 + c:8 + d * 4 + c + 1]
                    ng_ = negh[:, d * 4 + c:d * 4 + c + 1]
                    for ci, (a0, nn) in enumerate(order):
                        for s0 in range(0, nn, 256):
                            def s_mm(d=d, a0=a0, s0=s0):
                                k.mm(Bbank[:, 0:256], wbd[:, d * 2 + 0, :], ub[:, a0 + s0:a0 + s0 + 256])
                                k.mm(Bbank[:, 256:512], wbd[:, d * 2 + 1, :], ub[:, a0 + s0:a0 + s0 + 256])
                            add(s_mm)

                            def s_tanh(s0=s0, ba_=ba_, bx_=bx_):
                                k.act(tra[:, s0:s0 + 256], Bbank[:, 0:256], AF.Tanh, bias=ba_, scale=0.5)
                                k.act(tib[:, s0:s0 + 256], Bbank[:, 256:512], AF.Tanh, bias=bx_, scale=0.5)
                            add(s_tanh)

                        def s_e1(nn=nn, a0=a0, ng_=ng_):
                            k.act(tra[:, 0:nn], tra[:, 0:nn], AF.Exp, bias=ng_, scale=ng_)
                            k.stt("dve", tib[:, 0:nn], tib[:, 0:nn], 1.0, ub[:, a0:a0 + nn], ALU.add, ALU.mult)
                        add(s_e1)

                        def s_e2(nn=nn):
                            k.tt("pool", hd[:, 0:nn], tra[:, 0:nn], tra[:, 0:nn], ALU.mult)
                        add(s_e2)

                        def s_e3(nn=nn):
                            k.act(hd[:, 0:nn], hd[:, 0:nn], AF.Sqrt, bias=0.25, scale=-0.25)
                        add(s_e3)

                        def s_e4(nn=nn):
                            k.tt("pool", tib[:, 0:nn], tib[:, 0:nn], hd[:, 0:nn], ALU.mult)
                        add(s_e4)

                        def s_e5(nn=nn, a0=a0, d=d, ci=ci, c=c):
                            init = 0.0 if ci == 0 else carry[:, d:d + 1]
                            if d == 0:
                                k.scan("dve", hd[:, 0:nn], tra[:, 0:nn], tib[:, 0:nn], init, ALU.mult, ALU.add)
                                k.copy("dve", carry[:, 0:1], hd[:, nn - 1:nn])
                                k.copy("pool", hfb[:, a0:a0 + nn], hd[:, 0:nn])
                            else:
                                k.scan("dve", hd[:, 0:nn][:, ::-1], tra[:, 0:nn][:, ::-1], tib[:, 0:nn][:, ::-1], init,
                                       ALU.mult, ALU.add)
                                k.copy("dve", carry[:, 1:2], hd[:, 0:1])
                                k.tt("pool", hd[:, 0:nn], hd[:, 0:nn], hfb[:, a0:a0 + nn], ALU.add)
                                k.tt("pool", zst[:, 0:nn], hd[:, 0:nn], lxg[:, a0:a0 + nn], ALU.mult)
                                k.dma("sp", za_s[c * 128:(c + 1) * 128, a0:a0 + nn], zst[:, 0:nn])
                        add(s_e5)

            bpos = [0]

            def bstep(cnt=1):
                for _ in range(cnt):
                    if bpos[0] >= len(steps):
                        return False
                    steps[bpos[0]]()
                    bpos[0] += 1
                return True

            vhs = [k.sb(P, f"vh{i}", [128, NKT, 65], BF16) for i in range(2)]
            kTs = [k.sb(P, f"kT{i}", [96, T], BF16) for i in range(2)]
            qTs = [k.sb(P, f"qT{i}", [96, T], BF16) for i in range(2)]
            pT = [k.sb(P, f"pT{i}", [128, 2, 512], BF16) for i in range(3)]
            osb = k.sb(P, "osb", [128, 512], F32)
            rden = k.sb(P, "rden", [128, 512], F32)
            bcs = k.sb(P, "bcs", [64, 512], F32)
            onesf = k.sb(P, "onesf", [128, 64], F32)
            st_at = [k.sb(P, f"st_at{i}", [64, 512], BF16) for i in range(2)]
            k.memset("dve", onesf[:], 1.0)
            scale = 96.0 ** -0.5
            qblocks = [(t0, n, is_ctx) for (t0, n, is_ctx) in tiles if not (last and is_ctx)]
            items = []
            for h in range(8):
                for bi, (t0, n, is_ctx) in enumerate(qblocks):
                    kts = list(range(NKT)) if not is_ctx else list(range(S // 128, NKT))
                    for j in range(0, len(kts), 2):
                        items.append((h, bi, t0, n, kts[j], kts[j + 1], j == 0, j == len(kts) - 2))
            if os.environ.get('K_NOATT'):
                items = []
            Sslot = [PB[0], PB[1], PB[2]]
            ops = PS[6]
            slot_ctr = [0]
            LA = 2
            loaded = set()

            def load_head(h):
                if h < 8 and h not in loaded:
                    loaded.add(h)
                    k.dma("sp", kTs[h % 2][:], k_s[h])
                    k.dma("sp", qTs[h % 2][:], q_s[h])
                    k.dma("sp", vhs[h % 2][:],
                          v_s.rearrange("(kt p) (h e) -> p kt h e", p=128, e=65)[:, :, h, :])

            load_head(0)
            nblk = [0]
            if os.environ.get('K_BFIRST'):
                while bstep(1):
                    pass
            for idx in range(len(items) + LA):
                if idx < len(items):
                    h, bi, t0, n, kt0, kt1, first, lastp = items[idx]
                    kT = kTs[h % 2]
                    qT = qTs[h % 2]
                    sl = Sslot[slot_ctr[0] % 3]
                    slot_ctr[0] += 1
                    k.mm(sl[:, 0:n], kT[:, kt0 * 128:(kt0 + 1) * 128], qT[:, t0:t0 + n])
                    k.mm(sl[:, 512:512 + n], kT[:, kt1 * 128:(kt1 + 1) * 128], qT[:, t0:t0 + n])
                    p_ = pT[idx % 3]
                    k.act(p_[:, :, 0:n], sl[:].rearrange("p (a b) -> p a b", b=512)[:, :, 0:n], AF.Exp, scale=scale)
                if idx >= LA:
                    j = idx - LA
                    h, bi, t0, n, kt0, kt1, first, lastp = items[j]
                    if first and bi == 0:
                        load_head(h + 1)
                    vh = vhs[h % 2]
                    p_ = pT[j % 3]
                    k.mm(ops[0:65, 0:n], vh[:, kt0, :], p_[:, 0, 0:n], start=first, stop=False)
                    k.mm(ops[0:65, 0:n], vh[:, kt1, :], p_[:, 1, 0:n], start=False, stop=lastp)
                    if lastp:
                        bno = nblk[0]
                        nblk[0] += 1
                        k.copy("dve", osb[0:65, 0:n], ops[0:65, 0:n])
                        k.recip(rden[64:65, 0:n], osb[64:65, 0:n])
                        bps = Sslot[slot_ctr[0] % 3]
                        slot_ctr[0] += 1
                        k.mm(bps[0:64, 0:n], onesf[64:65, 0:64], rden[64:65, 0:n])
                        k.copy("dve", bcs[:, 0:n], bps[0:64, 0:n])
                        sa = st_at[bno % 2]
                        k.tt("pool", sa[:, 0:n], osb[0:64, 0:n], bcs[:, 0:n], ALU.mult)
                        k.dma("sp", at_s[h * 64:(h + 1) * 64, t0:t0 + n], sa[:, 0:n])
                if not os.environ.get('K_NOB'):
                    target = int(len(steps) * min(1.0, (idx + 1) / max(1.0, 0.9 * len(items)))) if items else len(steps)
                    while bpos[0] < target:
                        bstep(1)
            while (not os.environ.get('K_NOB')) and bstep(1):
                pass
        k.barrier()

        with ExitStack() as P:
            dg = k.sb(P, "dg", [128, 4, 31, 128], BF16)
            for c in range(4):
                for j in range(31):
                    k.ts("pool" if (j % 2) else "dve", dg[:, c, j, :], ident[:],
                         vec[:, V_CW + c * 31 + j:V_CW + c * 31 + j + 1], None, ALU.mult)
            gin = [k.sb(P, f"gin{i}", [128, 4, 542], BF16) for i in range(2)]
            hc = k.sb(P, "hc", [128, 4, 512], F32)
            hcb = k.sb(P, "hcb", [128, 4, 512], BF16)
            hsq = k.sb(P, "hsq", [128, 4, 512], BF16)
            mu = k.sb(P, "mu", [128, 512], F32)
            msq = k.sb(P, "msq", [128, 512], F32)
            rs = k.sb(P, "rsd", [128, 512], F32)
            tmp = k.sb(P, "tmpd", [128, 512], F32)
            st_zc = k.sb(P, "st_zc", [128, 4, 512], BF16)
            for ti, (t0, n, is_ctx) in enumerate(act_tiles):
                g_ = gin[ti % 2]
                c0 = glucol(t0, is_ctx) - 15
                k.dma("sp", g_[:, :, 0:n + 30], fm(glu_s)[:, :, c0:c0 + n + 30])
                for c in range(4):
                    ps = PS[c]
                    for j in range(31):
                        k.mm(ps[:, 0:n], dg[:, c, j, :], g_[:, c, j:j + n], start=(j == 0), stop=(j == 30))
                    k.act(hc[:, c, 0:n], ps[:, 0:n], AF.Identity, bias=vec[:, V_CB + c:V_CB + c + 1], scale=1.0)
                    k.copy("pool", hcb[:, c, 0:n], hc[:, c, 0:n])
                    k.act(hsq[:, c, 0:n], hc[:, c, 0:n], AF.Square)
                for c in range(4):
                    k.mm(PS[4][:, 0:n], ones[:], hcb[:, c, 0:n], start=(c == 0), stop=(c == 3))
                for c in range(4):
                    k.mm(PS[5][:, 0:n], ones[:], hsq[:, c, 0:n], start=(c == 0), stop=(c == 3))
                k.ts("dve", mu[:, 0:n], PS[4][:, 0:n], 1.0 / 512, None, ALU.mult)
                k.tt("dve", msq[:, 0:n], mu[:, 0:n], mu[:, 0:n], ALU.mult)
                k.stt("dve", rs[:, 0:n], PS[5][:, 0:n], 1.0 / 512, msq[:, 0:n], ALU.mult, ALU.subtract)
                k.act(rs[:, 0:n], rs[:, 0:n], AF.Ln, bias=EPS, scale=1.0)
                k.act(rs[:, 0:n], rs[:, 0:n], AF.Exp, scale=-0.5)
                for c in range(4):
                    k.tt("pool", tmp[:, 0:n], hc[:, c, 0:n], mu[:, 0:n], ALU.subtract)
                    k.tt("dve", tmp[:, 0:n], tmp[:, 0:n], rs[:, 0:n], ALU.mult)
                    k.act(st_zc[:, c, 0:n], tmp[:, 0:n], AF.Silu, bias=vec[:, V_LNB + c:V_LNB + c + 1],
                          scale=vec[:, V_LNG + c:V_LNG + c + 1])
                k.dma("sp", fm(zc_s)[:, :, t0:t0 + n], st_zc[:, :, 0:n])
        k.barrier()

        with ExitStack() as P:
            wbo = k.sb(P, "wbo", [128, 3, 4, D], BF16)
            for bi, nm in enumerate(("w_lru_o", "w_mla_o", "w_conf_o")):
                k.dma("pool", wbo[:, bi], fm(W[nm][l]))
            wo = k.sb(P, "wo", [128, 8, D], BF16)
            k.dma("pool", wo[:], fm(W["w_out"][l]))
            zin = [k.sb(P, f"zin{i}", [128, 3, 4, 512], BF16) for i in range(2)]
            gt = [k.sb(P, f"gt{i}", [128, 24, 512], BF16) for i in range(2)]
            xts = [k.sb(P, f"xe{i}", [128, 8, 512], F32) for i in range(2)]
            mm_ = k.sb(P, "mmix", [128, 8, 512], BF16)
            t1s = [k.sb(P, f"t1{i}", [128, 512], F32) for i in range(2)]
            t2s = [k.sb(P, f"t2{i}", [128, 512], F32) for i in range(2)]
            t3s = [k.sb(P, f"t3{i}", [128, 512], F32) for i in range(2)]
            xo = k.sb(P, "xo", [128, 8, 512], F32)
            for ti, (t0, n, is_ctx) in enumerate(act_tiles):
                w2 = 1 if is_ctx else 0
                z_ = zin[ti % 2]
                g_ = gt[ti % 2]
                xt = xts[ti % 2]
                k.dma("sp", z_[:, 0, :, 0:n], fm(za_s)[:, :, t0:t0 + n])
                k.dma("sp", z_[:, 1, :, 0:n], fm(at_s)[:, :, t0:t0 + n])
                k.dma("sp", z_[:, 2, :, 0:n], fm(zc_s)[:, :, t0:t0 + n])
                k.dma("sp", g_[:, :, 0:n], fm(gate_s)[:, :, t0:t0 + n])
                k.dma("sp", xt[:, :, 0:n], fm(x_src)[:, :, t0:t0 + n])
                for m in range(8):
                    pss = [PS[(3 * m + bi) % 6] for bi in range(3)]
                    for bi in range(3):
                        for kc in range(4):
                            k.mm(pss[bi][:, 0:n], wbo[:, bi, kc, m * 128:(m + 1) * 128], z_[:, bi, kc, 0:n],
                                 start=(kc == 0), stop=(kc == 3))
                    t1, t2, t3 = t1s[m % 2], t2s[m % 2], t3s[m % 2]
                    k.tt("dve", t1[:, 0:n], pss[0][:, 0:n], g_[:, m, 0:n], ALU.mult)
                    k.tt("dve", t2[:, 0:n], pss[1][:, 0:n], g_[:, 8 + m, 0:n], ALU.mult)
                    k.tt("dve", t3[:, 0:n], pss[2][:, 0:n], g_[:, 16 + m, 0:n], ALU.mult)
                    k.tt("pool", t1[:, 0:n], t1[:, 0:n], t2[:, 0:n], ALU.add)
                    k.tt("pool", mm_[:, m, 0:n], t1[:, 0:n], t3[:, 0:n], ALU.add)
                for m in range(8):
                    ps = PS[6 + m % 2]
                    for kc in range(8):
                        k.mm(ps[:, 0:n], wo[:, kc, m * 128:(m + 1) * 128], mm_[:, kc, 0:n], start=(kc == 0), stop=(kc == 7))
                    k.stt("dve", xo[:, m, 0:n], ps[:, 0:n], mod[:, 16 + m, w2:w2 + 1], xt[:, m, 0:n], ALU.mult, ALU.add)
                k.dma("sp", fm(xb)[:, :, t0:t0 + n], xo[:, :, 0:n])
        k.barrier()

        with ExitStack() as P:
            w1 = k.sb(P, "w1", [128, 8, 4 * D], BF16)
            w2t = k.sb(P, "w2t", [128, 32, D], BF16)
            for piece in range(8):
                k.dma("pool", w1[:, piece, :], W["w_ff1"][l][piece * 128:(piece + 1) * 128, :])
            for piece in range(8):
                k.dma("pool", w2t[:, piece * 4:(piece + 1) * 4, :],
                      fm(W["w_ff2"][l])[:, piece * 4:(piece + 1) * 4, :])
            xts = [k.sb(P, f"xf{i}", [128, 8, 512], F32) for i in range(2)]
            hff = k.sb(P, "hff", [128, 32, 512], BF16)
            h2 = k.sb(P, "h2", [128, 8, 512], BF16)
            rstd = k.sb(P, "rstd2", [128, 512], F32)
            rl = k.sb(P, "rl", [128, 512], F32)
            for ti, (t0, n, is_ctx) in enumerate(act_tiles):
                w2 = 1 if is_ctx else 0
                xt = xts[ti % 2]
                k.dma("sp", xt[:, :, 0:n], fm(xb)[:, :, t0:t0 + n])
                norm_mod(xt, n, w2, 24, 32, hff[:, 0:8, :], rstd, h2, PS[0])
                for j in range(32):
                    ps = PS[1 + j % 4]
                    for kc in range(8):
                        k.mm(ps[:, 0:n], w1[:, kc, j * 128:(j + 1) * 128], h2[:, kc, 0:n], start=(kc == 0), stop=(kc == 7))
                    k.act(rl[:, 0:n], ps[:, 0:n], AF.Relu)
                    k.tt("pool", hff[:, j, 0:n], rl[:, 0:n], rl[:, 0:n], ALU.mult)
                for m in range(8):
                    ps = PS[5 + m % 3]
                    for kc in range(32):
                        k.mm(ps[:, 0:n], w2t[:, kc, m * 128:(m + 1) * 128], hff[:, kc, 0:n], start=(kc == 0), stop=(kc == 31))
                    k.stt("dve", xt[:, m, 0:n], ps[:, 0:n], mod[:, 40 + m, w2:w2 + 1], xt[:, m, 0:n], ALU.mult, ALU.add)
                if l == DEPTH - 1:
                    if not is_ctx:
                        k.dma("sp", fm(outT)[:, :, t0:t0 + n], xt[:, :, 0:n])
                else:
                    k.dma("sp", fm(xa)[:, :, t0:t0 + n], xt[:, :, 0:n])
        k.barrier()
        L.close()

    k.barrier()
    return nc, k


def _fmcols(v):
    return np.ascontiguousarray(v.reshape(-1, 128).T)


def _pad96(v):
    o = np.zeros((128, 1), np.float32)
    o[:96, 0] = v
    return o


def _partner(g):
    p = g.copy()
    p[64:80] = g[80:96]
    p[80:96] = g[64:80]
    return p


def prep_vecs(inp, depth):
    out = np.zeros((depth, 128, NV), np.float32)
    for l in range(depth):
        cols = []
        cols.append(_fmcols(inp["b_ada"][l]))
        cols.append(np.ascontiguousarray(inp["lru_conv_w"][l].reshape(4, 4, 128).transpose(2, 1, 0)).reshape(128, 16))
        cols.append(_fmcols(inp["lru_conv_b"][l]))
        for nm in ("lru_ba", "lru_bx", "lru_lambda"):
            cols.append(np.ascontiguousarray(inp[nm][l].reshape(2, 4, 128).transpose(2, 0, 1)).reshape(128, 8))
        cols.append(_fmcols(inp["mla_q_norm"][l]))
        cols.append(_fmcols(inp["mla_kv_norm"][l]))
        cols.append(_pad96(inp["mla_q_gain"][l]))
        cols.append(_pad96(_partner(inp["mla_q_gain"][l])))
        cols.append(_pad96(inp["mla_k_gain"][l]))
        cols.append(_pad96(_partner(inp["mla_k_gain"][l])))
        cols.append(np.ascontiguousarray(inp["conf_dw_w"][l].reshape(31, 4, 128).transpose(2, 1, 0)).reshape(128, 124))
        cols.append(_fmcols(inp["conf_dw_b"][l]))
        cols.append(_fmcols(inp["conf_ln_g"][l]))
        cols.append(_fmcols(inp["conf_ln_b"][l]))
        out[l] = np.concatenate(cols, axis=1)
    return out


def rope_table(S):
    T = S + CTX
    rows = S // 64
    row = np.repeat(np.arange(rows, dtype=np.float32), 64)
    col = np.tile(np.arange(64, dtype=np.float32), rows)
    half = 16
    freqs = (np.float32(10000.0) ** (-np.arange(0, half, 2, dtype=np.float32) / np.float32(half))).astype(np.float32)
    ang = np.concatenate([row[:, None] * freqs, col[:, None] * freqs], axis=-1).astype(np.float32)
    cos = np.cos(ang).astype(np.float32).T
    sin = np.sin(ang).astype(np.float32).T
    tab = np.zeros((32, 2, T), np.float32)
    tab[:, 0, S:] = 1.0
    tab[0:16, 0, :S] = cos
    tab[16:32, 0, :S] = cos
    tab[0:16, 1, :S] = -sin
    tab[16:32, 1, :S] = sin
    return tab


def make_in_maps(inp, S, depth, nb):
    vecs = prep_vecs(inp, depth)
    rope = rope_table(S)
    ident = np.eye(128, dtype=np.float32)
    shared = {"vecs": vecs, "rope": rope, "ident": ident}
    for nm in WNAMES:
        shared[nm] = np.ascontiguousarray(inp[nm][:depth])
    maps = []
    for b in range(nb):
        xT = np.ascontiguousarray(np.concatenate([inp["x"][b].T, inp["ctx"][b].T], axis=1))
        cvec = np.ascontiguousarray(np.stack([_fmcols(inp["c"][b]), _fmcols(inp["c_ctx"])], axis=-1).reshape(128, 16))
        m = dict(shared)
        m["xT"] = xT
        m["cvec"] = cvec
        maps.append(m)
    return maps


_CACHE = {}


def kernel(**inputs):
    inp = {k_: np.asarray(v) for k_, v in inputs.items()}
    B, S, _ = inp["x"].shape
    key = (S, DEPTH_FULL)
    if key not in _CACHE:
        _CACHE[key] = build(S, DEPTH_FULL)[0]
    nc = _CACHE[key]
    maps = make_in_maps(inp, S, DEPTH_FULL, B)
    res = run_bass_kernel_spmd(nc, maps, core_ids=list(range(B)))
    out = np.stack([np.ascontiguousarray(r["outT"].T) for r in res.results], axis=0)
    return out.astype(np.float32)
```
